# Optimizing a Trainium2 kernel written in Bass

```python
import jax, jax.numpy as jnp
from jax import lax
import numpy as np

D_MODEL = 1024
BATCH = 8
SEQ = 2048
DEPTH = 4
DEC_BATCH = 1
DEC_SEQ = 16384
PAST_LEN = 128

N_MEM = 256
D_FF = 2816
FOURIER_GROUPS = 4
FOURIER_GROUP_DIM = 128
FOURIER_WIDTH = FOURIER_GROUPS * FOURIER_GROUP_DIM
RET_HEADS = 4
RET_DK = 128
RET_DV = 256
RET_QK_WIDTH = RET_HEADS * RET_DK
RET_V_WIDTH = RET_HEADS * RET_DV
RET_CHUNK = 128
N_BRANCHES = 2
IN_WIDTH = FOURIER_WIDTH + 2 * RET_QK_WIDTH + 2 * RET_V_WIDTH + N_BRANCHES * D_MODEL
IN_SPLITS = (FOURIER_WIDTH,
             FOURIER_WIDTH + RET_QK_WIDTH,
             FOURIER_WIDTH + 2 * RET_QK_WIDTH,
             FOURIER_WIDTH + 2 * RET_QK_WIDTH + RET_V_WIDTH,
             FOURIER_WIDTH + 2 * RET_QK_WIDTH + 2 * RET_V_WIDTH)
XA_HEADS = 4
XA_HEAD_DIM = D_MODEL // XA_HEADS
ROPE_BASE = 10000.0
EPS = 1e-6

kernel_name = "hybrid_fnet_retention_encoder"


def rmsnorm(x, g):
    xf = x.astype(jnp.float32)
    y = xf * lax.rsqrt(jnp.mean(xf * xf, axis=-1, keepdims=True) + EPS)
    return (y * g.astype(jnp.float32)).astype(x.dtype)


def swiglu(h, w_in, w_out):
    gate, up = jnp.split(h @ w_in, 2, axis=-1)
    return (jax.nn.silu(gate) * up) @ w_out


def rotary(x, pos):
    d = x.shape[-1]
    half = d // 2
    inv = 1.0 / (ROPE_BASE ** (jnp.arange(half, dtype=jnp.float32) * 2.0 / d))
    ang = pos[:, None] * inv[None, :]
    cos = jnp.cos(ang)[None, :, None, :]
    sin = jnp.sin(ang)[None, :, None, :]
    x1, x2 = x[..., :half], x[..., half:]
    return jnp.concatenate([x1 * cos - x2 * sin, x2 * cos + x1 * sin], axis=-1)


def fourier_mix(hf):
    B, S, _ = hf.shape
    a = hf.astype(jnp.float32).reshape(B, S, FOURIER_GROUPS, FOURIER_GROUP_DIM)
    y = jnp.fft.fft2(a, axes=(1, 3), norm="ortho").real
    return y.reshape(B, S, FOURIER_WIDTH).astype(hf.dtype)


def retention_dir(q, k, v, log_gamma, strict):
    B, S, H, DK = q.shape
    DV = v.shape[-1]
    C = RET_CHUNK
    N = S // C
    qc = q.reshape(B, N, C, H, DK)
    kc = k.reshape(B, N, C, H, DK)
    vc = v.reshape(B, N, C, H, DV)
    idx = jnp.arange(C, dtype=jnp.float32)
    diff = idx[:, None] - idx[None, :]
    mask = (diff > 0) if strict else (diff >= 0)
    decay = jnp.where(mask[None], jnp.exp(log_gamma[:, None, None] * jnp.where(mask, diff, 0.0)[None]), 0.0)
    scores = jnp.einsum('bnchd,bnmhd->bnhcm', qc, kc) * decay[None, None]
    intra = jnp.einsum('bnhcm,bnmhe->bnche', scores, vc)
    xi = jnp.exp(log_gamma[None, :] * (idx[:, None] + 1.0))
    zeta = jnp.exp(log_gamma[None, :] * (C - 1.0 - idx[:, None]))
    g_chunk = jnp.exp(log_gamma * C)[None, :, None, None]
    upd = jnp.einsum('bnmhd,bnmhe->nbhde', kc * zeta[None, None, :, :, None], vc)

    def step(state, u):
        return g_chunk * state + u, state

    _, r_prev = lax.scan(step, jnp.zeros((B, H, DK, DV), jnp.float32), upd)
    cross = jnp.einsum('bnchd,nbhde->bnche', qc * xi[None, None, :, :, None], r_prev)
    return (intra + cross).reshape(B, S, H, DV)


def retention(hq, hk, hv, hg, decay_fwd, decay_bwd):
    B, S, _ = hq.shape
    pos = jnp.arange(S, dtype=jnp.float32)
    q = rotary(hq.astype(jnp.float32).reshape(B, S, RET_HEADS, RET_DK), pos)
    k = rotary(hk.astype(jnp.float32).reshape(B, S, RET_HEADS, RET_DK), pos) * (RET_DK ** -0.5)
    v = hv.astype(jnp.float32).reshape(B, S, RET_HEADS, RET_DV)
    lg_f = jax.nn.log_sigmoid(decay_fwd.astype(jnp.float32))
    lg_b = jax.nn.log_sigmoid(decay_bwd.astype(jnp.float32))
    y_f = retention_dir(q, k, v, lg_f, False)
    y_b = jnp.flip(retention_dir(jnp.flip(q, 1), jnp.flip(k, 1), jnp.flip(v, 1), lg_b, True), 1)
    y = y_f + y_b
    y = y * lax.rsqrt(jnp.mean(y * y, axis=-1, keepdims=True) + EPS)
    y = y.reshape(B, S, RET_V_WIDTH).astype(hg.dtype)
    return jax.nn.silu(hg) * y


def cross_attention(h, mem_n, wq, wkv, wo):
    B, S, _ = h.shape
    M = mem_n.shape[1]
    q = (h @ wq).reshape(B, S, XA_HEADS, XA_HEAD_DIM)
    k, v = jnp.split(mem_n @ wkv, 2, axis=-1)
    k = k.reshape(B, M, XA_HEADS, XA_HEAD_DIM)
    v = v.reshape(B, M, XA_HEADS, XA_HEAD_DIM)
    s = jnp.einsum('bshd,bmhd->bhsm', q, k).astype(jnp.float32) * (XA_HEAD_DIM ** -0.5)
    p = jax.nn.softmax(s, axis=-1).astype(v.dtype)
    o = jnp.einsum('bhsm,bmhd->bshd', p, v).reshape(B, S, D_MODEL)
    return o @ wo


def trunk(x, mem, ffn1_norm, ffn1_w_in, ffn1_w_out, mix_norm, mix_w_in, fourier_w, ret_decay_fwd,
          ret_decay_bwd, ret_w_out, mix_w_out, xa_norm, mem_norm, xa_wq, xa_wkv, xa_wo,
          ffn2_norm, ffn2_w_in, ffn2_w_out, final_norm):
    for l in range(DEPTH):
        x = x + 0.5 * swiglu(rmsnorm(x, ffn1_norm[l]), ffn1_w_in[l], ffn1_w_out[l])
        h = rmsnorm(x, mix_norm[l])
        hf, hq, hk, hv, hg, gates = jnp.split(h @ mix_w_in[l], IN_SPLITS, axis=-1)
        y_a = fourier_mix(hf) @ fourier_w[l]
        y_b = retention(hq, hk, hv, hg, ret_decay_fwd[l], ret_decay_bwd[l]) @ ret_w_out[l]
        g_a, g_b = jnp.split(jax.nn.sigmoid(gates), 2, axis=-1)
        x = x + (g_a * y_a + g_b * y_b) @ mix_w_out[l]
        x = x + cross_attention(rmsnorm(x, xa_norm[l]), rmsnorm(mem, mem_norm[l]), xa_wq[l], xa_wkv[l], xa_wo[l])
        x = x + 0.5 * swiglu(rmsnorm(x, ffn2_norm[l]), ffn2_w_in[l], ffn2_w_out[l])
    return rmsnorm(x, final_norm)


def setup_inputs(seed: int = 0) -> dict:
    key = jax.random.key(seed)
    ks = jax.random.split(key, 32)
    f32 = jnp.float32

    def w(k, shape, fan_in):
        return jax.random.normal(k, shape, f32) * (fan_in ** -0.5)

    def gain(k, shape):
        return 1.0 + 0.01 * jax.random.normal(k, shape, f32)

    base_logit = jnp.log(2.0 ** (5.0 + jnp.arange(RET_HEADS, dtype=f32)) - 1.0)
    return {
        "x_prompt": jax.random.normal(ks[0], (BATCH, SEQ, D_MODEL), f32),
        "x_sample": jax.random.normal(ks[1], (DEC_BATCH, DEC_SEQ, D_MODEL), f32),
        "mem_prompt": jax.random.normal(ks[2], (BATCH, N_MEM, D_MODEL), f32),
        "mem_sample": jax.random.normal(ks[3], (DEC_BATCH, N_MEM, D_MODEL), f32),
        "ffn1_norm": gain(ks[4], (DEPTH, D_MODEL)),
        "ffn1_w_in": w(ks[5], (DEPTH, D_MODEL, 2 * D_FF), D_MODEL),
        "ffn1_w_out": w(ks[6], (DEPTH, D_FF, D_MODEL), D_FF),
        "mix_norm": gain(ks[7], (DEPTH, D_MODEL)),
        "mix_w_in": w(ks[8], (DEPTH, D_MODEL, IN_WIDTH), D_MODEL),
        "fourier_w": w(ks[9], (DEPTH, FOURIER_WIDTH, D_MODEL), FOURIER_WIDTH),
        "ret_decay_fwd": base_logit[None, :] + 0.05 * jax.random.normal(ks[10], (DEPTH, RET_HEADS), f32),
        "ret_decay_bwd": base_logit[None, :] + 0.05 * jax.random.normal(ks[11], (DEPTH, RET_HEADS), f32),
        "ret_w_out": w(ks[12], (DEPTH, RET_V_WIDTH, D_MODEL), RET_V_WIDTH),
        "mix_w_out": w(ks[13], (DEPTH, D_MODEL, D_MODEL), D_MODEL),
        "xa_norm": gain(ks[14], (DEPTH, D_MODEL)),
        "mem_norm": gain(ks[15], (DEPTH, D_MODEL)),
        "xa_wq": w(ks[16], (DEPTH, D_MODEL, D_MODEL), D_MODEL),
        "xa_wkv": w(ks[17], (DEPTH, D_MODEL, 2 * D_MODEL), D_MODEL),
        "xa_wo": w(ks[18], (DEPTH, D_MODEL, D_MODEL), D_MODEL),
        "ffn2_norm": gain(ks[19], (DEPTH, D_MODEL)),
        "ffn2_w_in": w(ks[20], (DEPTH, D_MODEL, 2 * D_FF), D_MODEL),
        "ffn2_w_out": w(ks[21], (DEPTH, D_FF, D_MODEL), D_FF),
        "final_norm": gain(ks[22], (D_MODEL,)),
    }


def reference(x_prompt, x_sample, mem_prompt, mem_sample, ffn1_norm, ffn1_w_in, ffn1_w_out, mix_norm,
              mix_w_in, fourier_w, ret_decay_fwd, ret_decay_bwd, ret_w_out, mix_w_out, xa_norm, mem_norm,
              xa_wq, xa_wkv, xa_wo, ffn2_norm, ffn2_w_in, ffn2_w_out, final_norm):
    y_prompt = trunk(x_prompt, mem_prompt, ffn1_norm, ffn1_w_in, ffn1_w_out, mix_norm, mix_w_in, fourier_w,
                     ret_decay_fwd, ret_decay_bwd, ret_w_out, mix_w_out, xa_norm, mem_norm, xa_wq, xa_wkv,
                     xa_wo, ffn2_norm, ffn2_w_in, ffn2_w_out, final_norm)
    y_sample = trunk(x_sample, mem_sample, ffn1_norm, ffn1_w_in, ffn1_w_out, mix_norm, mix_w_in, fourier_w,
                     ret_decay_fwd, ret_decay_bwd, ret_w_out, mix_w_out, xa_norm, mem_norm, xa_wq, xa_wkv,
                     xa_wo, ffn2_norm, ffn2_w_in, ffn2_w_out, final_norm)
    return (y_prompt, y_sample)
```

```python
import contextlib
import math
import numpy as np
import concourse.bass as bass
import concourse.mybir as mybir
from concourse.bass_utils import run_bass_kernel_spmd

F32 = mybir.dt.float32
BF16 = mybir.dt.bfloat16
AF = mybir.ActivationFunctionType
ALU = mybir.AluOpType
AX = mybir.AxisListType

D = 1024
KC = 8
T = 2048
TT = 512
NT = T // TT
NCH = T // 128
DEPTH = 4
DFF = 2816
NF = DFF // 128
NH = NF // 2
NMEM = 256
EPS = 1e-6
NCORES = 8
SSAMP = NCORES * T
LN_S = math.log(128.0 ** -0.5)
OFF_HF, OFF_Q, OFF_K, OFF_V, OFF_G, OFF_GA, OFF_GB = 0, 512, 1024, 1536, 2560, 3584, 4608

WNAMES = ["ffn1_w_in", "ffn1_w_out", "mix_w_in", "fourier_w", "ret_w_out", "mix_w_out",
          "xa_wq", "xa_wkv", "xa_wo", "ffn2_w_in", "ffn2_w_out"]
WSHAPES = {"ffn1_w_in": (D, 2 * DFF), "ffn1_w_out": (DFF, D), "mix_w_in": (D, 5632),
           "fourier_w": (512, D), "ret_w_out": (D, D), "mix_w_out": (D, D), "xa_wq": (D, D),
           "xa_wkv": (D, 2 * D), "xa_wo": (D, D), "ffn2_w_in": (D, 2 * DFF), "ffn2_w_out": (DFF, D)}
G_FFN1, G_MIX, G_XA, G_MEM, G_FFN2 = range(5)
AW = 512
NATOM = 80


class Atom:
    __slots__ = ("w", "r")

    def __init__(self):
        self.w = None
        self.r = {}


def atoms(n):
    return [Atom() for _ in range(n)]


class Eng:
    def __init__(self, name, sem, is_pe=False):
        self.name = name
        self.sem = sem
        self.is_pe = is_pe
        self.ops = []
        self.cnt = 0
        self.known = {}


class Prog:
    def __init__(self, nc, sems):
        self.nc = nc
        self.sems = sems
        self.eng = {}
        self.free_sems = list(range(len(sems)))
        for n in ["pe", "act", "dve", "pool", "sp"]:
            self.eng[n] = Eng(n, self.free_sems.pop(0), is_pe=(n == "pe"))
        self.dma_ring = {"sp": [[self.free_sems.pop(0), 0] for _ in range(24)],
                         "pool": [[self.free_sems.pop(0), 0] for _ in range(24)]}
        self.dma_next = {"sp": 0, "pool": 0}
        self.cc_sem = self.free_sems.pop(0)
        self.cc_cnt = 0
        self.n_instr = 0

    def _deps(self, eng, reads, writes):
        deps = {}
        for a in reads:
            if a.w is not None:
                s, v = a.w
                if deps.get(s, 0) < v:
                    deps[s] = v
        for a in writes:
            if a.w is not None:
                s, v = a.w
                if deps.get(s, 0) < v:
                    deps[s] = v
            for s, v in a.r.items():
                if deps.get(s, 0) < v:
                    deps[s] = v
        for s, v in deps.items():
            if s == eng.sem and eng.is_pe:
                continue
            if eng.known.get(s, 0) >= v:
                continue
            eng.known[s] = v
            eng.ops.append(("wait", s, v))

    def op(self, ename, fn, reads=(), writes=()):
        eng = self.eng[ename]
        self._deps(eng, reads, writes)
        eng.cnt += 1
        eng.ops.append(("op", fn))
        self.n_instr += 1
        for a in reads:
            a.r[eng.sem] = eng.cnt
        for a in writes:
            a.w = (eng.sem, eng.cnt)
            a.r = {}

    def dma(self, qname, out_ap, in_ap, reads=(), writes=()):
        q = self.eng[qname]
        ring = self.dma_ring[qname]
        slot = ring[self.dma_next[qname]]
        self.dma_next[qname] = (self.dma_next[qname] + 1) % len(ring)
        s = slot[0]
        self._deps(q, reads, writes)
        if slot[1] > 0 and q.known.get(s, 0) < 16 * slot[1]:
            q.known[s] = 16 * slot[1]
            q.ops.append(("wait", s, 16 * slot[1]))
        slot[1] += 1
        v = 16 * slot[1]
        q.ops.append(("dma", out_ap, in_ap, s))
        self.n_instr += 1
        for a in reads:
            a.r[s] = v
        for a in writes:
            a.w = (s, v)
            a.r = {}

    def allgather(self, out_ap, in_ap, reads, writes):
        q = self.eng["pool"]
        self._deps(q, reads, writes)
        self.cc_cnt += 1
        v = self.cc_cnt
        q.ops.append(("cc", out_ap, in_ap, self.cc_sem))
        for a in reads:
            a.r[self.cc_sem] = v
        for a in writes:
            a.w = (self.cc_sem, v)
            a.r = {}

    def finish(self, out_atoms):
        sp = self.eng["sp"]
        self._deps(sp, out_atoms, ())

    def replay(self, ename, e):
        eng = self.eng[ename]
        sems = self.sems
        for o in eng.ops:
            k = o[0]
            if k == "wait":
                e.wait_ge(sems[o[1]], o[2])
            elif k == "op":
                o[1](e).then_inc(sems[eng.sem], 1)
            elif k == "dma":
                e.dma_start(out=o[1], in_=o[2]).then_inc(sems[o[3]], 16)
            elif k == "cc":
                e.collective_compute("AllGather", mybir.AluOpType.bypass,
                                     replica_groups=[list(range(NCORES))],
                                     ins=[o[2]], outs=[o[1]]).then_inc(sems[o[3]])


class AV:
    def __init__(self, arena, aat, a0, shape):
        self.aat = aat
        self.base = a0 * AW
        self.shape = shape
        n = int(np.prod(shape))
        assert self.base + n <= NATOM * AW, (a0, shape)
        flat = arena[:, self.base:self.base + n]
        if len(shape) == 1:
            self.ap = flat
            self.inner = shape[0]
        else:
            self.ap = flat.rearrange("p (a b) -> p a b", b=shape[1])
            self.inner = shape[1]

    def at(self, r0=0, r1=None, e0=0, e1=None):
        if len(self.shape) == 1:
            r0, r1 = 0, 1
        elif r1 is None:
            r1 = r0 + 1
        if e1 is None:
            e1 = self.inner
        out = []
        seen = set()
        for r in range(r0, r1):
            lo = self.base + r * self.inner + e0
            hi = self.base + r * self.inner + e1 - 1
            for a in range(lo // AW, hi // AW + 1):
                if a not in seen:
                    seen.add(a)
                    out.append(self.aat[a])
        return out

    def all(self):
        if len(self.shape) == 1:
            return self.at()
        return self.at(0, self.shape[0])


def build_nc(n_layers=DEPTH, groups=("p", "s"), do_ffn=True, do_mix=True, do_xa=True,
             do_fourier=True, do_ret=True):
    nc = bass.Bass("TRN2", target_bir_lowering=False)
    dr = {}

    def din(name, shape, dt=F32):
        dr[name] = nc.dram_tensor(name, list(shape), dt, kind="ExternalInput").ap()
        return dr[name]

    for g in "ps":
        din("xT_" + g, [D, T])
        din("memT_" + g, [D, NMEM])
        din("rot_" + g, [2, 128, T])
    for w in WNAMES:
        din(w, [DEPTH, WSHAPES[w][0], WSHAPES[w][1]])
    din("gains", [128, 21 * KC])
    din("decay", [1, 32])
    din("ones_bf", [128, 128], BF16)
    din("ident_bf", [128, 128], BF16)
    din("fc_bf", [128, 256], BF16)
    din("d0", [128, TT])
    din("xidx", [128, TT])
    din("tidx", [128, 16])
    din("zidx", [128, 2, 16])
    din("cidx", [128, 4, 8])
    din("E_p", [4, T, 2, TT], BF16)
    din("E_s", [4, SSAMP, 2, TT], BF16)
    yT = {g: nc.dram_tensor("yT_" + g, [D, T], F32, kind="ExternalOutput").ap() for g in "ps"}
    PSCR = nc.dram_tensor("pscr", [T, 1024], BF16)
    PGATH = nc.dram_tensor("pgath", [SSAMP, 1024], BF16)
    UBN = nc.dram_tensor("ubn", [128, 512], F32)
    UGATH = nc.dram_tensor("ugath", [NCORES * 128, 512], F32)

    with contextlib.ExitStack() as es:
        def sb(name, shape, dt):
            return es.enter_context(nc.sbuf_tensor(name, list(shape), dt))

        X = sb("X", [128, KC, T], F32)
        XN = sb("XN", [128, KC, T], BF16)
        ARENA = sb("ARENA", [128, NATOM * AW], BF16)
        NW = 4
        WR = sb("WR", [128, NW, KC, 128], BF16)
        RS = sb("RS", [128, 2, TT], F32)
        NTMP = 3
        TMP = sb("TMP", [128, NTMP, TT], F32)
        ROTB = sb("ROTB", [128, 1, 2, TT], F32)
        GA = sb("GA", [128, 21 * KC], F32)
        ONES = sb("ONES", [128, 128], BF16)
        IDENT = sb("IDENT", [128, 128], BF16)
        FC = sb("FC", [128, 256], BF16)
        D0 = sb("D0", [128, TT], F32)
        XIDX = sb("XIDX", [128, TT], F32)
        TIDX = sb("TIDX", [128, 16], F32)
        ZIDX = sb("ZIDX", [128, 2, 16], F32)
        CIDX = sb("CIDX", [128, 4, 8], F32)
        DEC = sb("DEC", [128, 32], F32)
        LG = sb("LG", [128, 32], F32)
        NLG = sb("NLG", [128, 32], F32)
        SM = sb("SM", [128, 128], F32)
        SM2 = sb("SM2", [128, 2, 16], F32)
        PS = [es.enter_context(nc.psum_tensor("ps%d" % i, [128, TT], F32)) for i in range(8)]
        sems = [es.enter_context(nc.semaphore("s%d" % i)) for i in range(64)]
        block = es.enter_context(nc.Block())

        P = Prog(nc, sems)
        aAR = atoms(NATOM)
        aX = [[Atom() for _ in range(NT)] for _ in range(KC)]
        aXN = atoms(NT)
        aWR = atoms(NW)
        aRS = atoms(2)
        aTMP = atoms(NTMP)
        aROT = atoms(1)
        aC = atoms(1)
        aLG = atoms(1)
        aSM = atoms(1)
        aSM2 = atoms(2)
        aPS = atoms(8)
        aOUT = atoms(1)
        aPSCR = atoms(NCH)
        aPG = atoms(1)
        aUB = atoms(1)
        aUG = atoms(1)
        st = {}

        def nxt(key, n):
            i = st.get(key, 0)
            st[key] = (i + 1) % n
            return i

        held = set()

        def nps(hold=False):
            while True:
                b = nxt("ps", 8)
                if b not in held:
                    break
            if hold:
                held.add(b)
            return b

        def ts(t):
            return slice(t * TT, (t + 1) * TT)

        def av(a0, shape):
            return AV(ARENA, aAR, a0, shape)

        HID = av(0, (NH, T))
        SQ = av(72, (KC, TT))
        HFY = av(0, (4, T))
        RT = av(16, (KC, T))
        MERGED = av(48, (KC, T))
        PPC = [av(16 + 8 * i, (4, 1024)) for i in range(3)]
        EPC = [av(40 + 16 * i, (16, TT)) for i in range(2)]
        PST = [av(72 + 2 * i, (1024,)) for i in range(2)]
        KTH = av(48, (T,))
        QTH = av(52, (T,))
        VH = av(56, (NCH, 256))
        TAB = av(64, (6, TT))
        ATB = [av(70 + i, (TT,)) for i in range(3)]
        QXF = av(73, (TT,))
        QXB = av(74, (TT,))
        RINB = av(75, (TT,))
        KZ = av(73, (4, 128))
        SQ2 = av(76, (2, TT))
        WV = av(64, (KC, 256))
        QX = av(0, (KC, T))
        KTX = av(32, (KC, NMEM))
        VX = av(36, (2, 1024))
        MEMN = av(40, (KC, NMEM))
        PB = [av(44 + 2 * i, (4, NMEM)) for i in range(2)]
        PTX = [av(48 + 2 * i, (KC, 128)) for i in range(2)]
        WS = av(52, (KC, TT))

        P.dma("sp", GA[:], dr["gains"], writes=aC)
        P.dma("sp", ONES[:], dr["ones_bf"], writes=aC)
        P.dma("sp", IDENT[:], dr["ident_bf"], writes=aC)
        P.dma("sp", FC[:], dr["fc_bf"], writes=aC)
        P.dma("sp", D0[:], dr["d0"], writes=aC)
        P.dma("sp", XIDX[:], dr["xidx"], writes=aC)
        P.dma("sp", TIDX[:], dr["tidx"], writes=aC)
        P.dma("sp", ZIDX[:], dr["zidx"], writes=aC)
        P.dma("sp", CIDX[:], dr["cidx"], writes=aC)
        P.dma("sp", DEC[:], dr["decay"].partition_broadcast(128), writes=aLG)
        P.op("act", lambda e: e.activation(out=DEC[:], in_=DEC[:], func=AF.Exp, scale=-1.0), reads=aLG, writes=aLG)
        P.op("act", lambda e: e.activation(out=NLG[:], in_=DEC[:], func=AF.Ln, bias=1.0), reads=aLG, writes=aLG)
        P.op("dve", lambda e: e.tensor_scalar(out=LG[:], in0=NLG[:], scalar1=-1.0, scalar2=None, op0=ALU.mult),
             reads=aLG, writes=aLG)

        def norm_stats(src_fn, n, atoms_src):
            P.op("act", lambda e: e.activation(out=SQ.ap[:, :, 0:n], in_=src_fn(), func=AF.Square),
                 reads=atoms_src, writes=SQ.all())
            b = nps()
            for k in range(KC):
                P.op("pe", lambda e, b=b, k=k: e.matmul(PS[b][:, 0:n], lhsT=ONES[:, :], rhs=SQ.ap[:, k, 0:n],
                                                      start=(k == 0), stop=(k == KC - 1)),
                     reads=SQ.all() + aC, writes=[aPS[b]])
            r = nxt("rs", 2)
            P.op("act", lambda e, b=b, r=r: e.activation(out=RS[:, r, 0:n], in_=PS[b][:, 0:n], func=AF.Ln,
                                                       scale=1.0 / D, bias=EPS),
                 reads=[aPS[b]], writes=[aRS[r]])
            P.op("act", lambda e, r=r: e.activation(out=RS[:, r, 0:n], in_=RS[:, r, 0:n], func=AF.Exp, scale=-0.5),
                 reads=[aRS[r]], writes=[aRS[r]])
            return r

        def rmsnorm_to_xn(gidx):
            for t in range(NT):
                xa = [aX[k][t] for k in range(KC)]
                r = norm_stats(lambda t=t: X[:, :, ts(t)], TT, xa)
                for k in range(KC):
                    col = gidx * KC + k
                    P.op("dve", lambda e, k=k, t=t, r=r, col=col: e.scalar_tensor_tensor(
                        out=XN[:, k, ts(t)], in0=X[:, k, ts(t)], scalar=GA[:, col:col + 1], in1=RS[:, r, :],
                        op0=ALU.mult, op1=ALU.mult),
                        reads=[aX[k][t], aRS[r]] + aC, writes=[aXN[t]])

        def load_w_block(wap, l, k0, nk, c0, swap=False):
            s = nxt("wr", NW)
            rows = wap[l, k0 * 128:(k0 + nk) * 128, :]
            if not swap:
                src = rows[:, c0:c0 + 128].rearrange("(k p) c -> p k c", p=128)
                P.dma("pool", WR[:, s, 0:nk, :], src, writes=[aWR[s]])
            else:
                P.dma("pool", WR[:, s, 0:nk, 0:64], rows[:, c0 + 64:c0 + 128].rearrange("(k p) c -> p k c", p=128),
                      writes=[aWR[s]])
                P.dma("pool", WR[:, s, 0:nk, 64:128], rows[:, c0:c0 + 64].rearrange("(k p) c -> p k c", p=128),
                      writes=[aWR[s]])
            return s

        def mm_group(b, s, nk, rhs_fn, rhs_atoms_fn, n=TT):
            for k in range(nk):
                P.op("pe", lambda e, b=b, s=s, k=k: e.matmul(
                    PS[b][:, 0:n], lhsT=WR[:, s, k, :], rhs=rhs_fn(k), start=(k == 0), stop=(k == nk - 1)),
                    reads=[aWR[s]] + rhs_atoms_fn(k), writes=[aPS[b]])

        def xn_rhs(t):
            return (lambda k: XN[:, k, ts(t)]), (lambda k: [aXN[t]])

        def x_add(b, mt, t, scale):
            P.op("dve", lambda e: e.scalar_tensor_tensor(
                out=X[:, mt, ts(t)], in0=PS[b][:, :], scalar=scale, in1=X[:, mt, ts(t)],
                op0=ALU.mult, op1=ALU.add),
                reads=[aPS[b], aX[mt][t]], writes=[aX[mt][t]])

        def ffn(l, w_in, w_out, gidx):
            rmsnorm_to_xn(l * 5 + gidx)
            for hh in range(2):
                for jj in range(NH):
                    j = hh * NH + jj
                    sg = load_w_block(w_in, l, 0, KC, j * 128)
                    su = load_w_block(w_in, l, 0, KC, DFF + j * 128)
                    for t in range(NT):
                        rf, ra = xn_rhs(t)
                        bg = nps()
                        mm_group(bg, sg, KC, rf, ra)
                        bu = nps()
                        mm_group(bu, su, KC, rf, ra)
                        m = nxt("tmp", NTMP)
                        P.op("act", lambda e, bg=bg, m=m: e.activation(out=TMP[:, m, :], in_=PS[bg][:, :], func=AF.Silu),
                             reads=[aPS[bg]], writes=[aTMP[m]])
                        P.op("dve", lambda e, bu=bu, m=m, jj=jj, t=t: e.tensor_tensor(
                            out=HID.ap[:, jj, ts(t)], in0=PS[bu][:, :], in1=TMP[:, m, :], op=ALU.mult),
                            reads=[aPS[bu], aTMP[m]], writes=HID.at(jj, e0=t * TT, e1=(t + 1) * TT))
                for mt in range(KC):
                    s1 = load_w_block(w_out, l, hh * NH, KC, mt * 128)
                    s2 = load_w_block(w_out, l, hh * NH + KC, NH - KC, mt * 128)
                    for t in range(NT):
                        b = nps()
                        for jj in range(NH):
                            s, kk = (s1, jj) if jj < KC else (s2, jj - KC)
                            P.op("pe", lambda e, b=b, s=s, kk=kk, jj=jj, t=t: e.matmul(
                                PS[b][:, :], lhsT=WR[:, s, kk, :], rhs=HID.ap[:, jj, ts(t)],
                                start=(jj == 0), stop=(jj == NH - 1)),
                                reads=[aWR[s]] + HID.at(jj, e0=t * TT, e1=(t + 1) * TT), writes=[aPS[b]])
                        x_add(b, mt, t, 0.5)

        def xa_block(l, g):
            rmsnorm_to_xn(l * 5 + G_XA)
            P.dma("pool", MEMN.ap[:, :, :], dr["memT_" + g].rearrange("(k p) m -> p k m", p=128), writes=MEMN.all())
            r = norm_stats(lambda: MEMN.ap[:, :, :], NMEM, MEMN.all())
            for k in range(KC):
                col = (l * 5 + G_MEM) * KC + k
                P.op("dve", lambda e, k=k, r=r, col=col: e.scalar_tensor_tensor(
                    out=MEMN.ap[:, k, :], in0=MEMN.ap[:, k, :], scalar=GA[:, col:col + 1], in1=RS[:, r, 0:NMEM],
                    op0=ALU.mult, op1=ALU.mult),
                    reads=MEMN.all() + [aRS[r]] + aC, writes=MEMN.all())
            for m in range(KC):
                s = load_w_block(dr["xa_wkv"], l, 0, KC, m * 128)
                b = nps()
                mm_group(b, s, KC, lambda k: MEMN.ap[:, k, :], lambda k: MEMN.all(), n=NMEM)
                P.op("act", lambda e, b=b, m=m: e.activation(out=KTX.ap[:, m, :], in_=PS[b][:, 0:NMEM], func=AF.Copy),
                     reads=[aPS[b]], writes=KTX.all())
            for n2 in range(2):
                P.dma("pool", WS.ap[:, :, :],
                      dr["xa_wkv"][l, :, D + n2 * TT:D + (n2 + 1) * TT].rearrange("(k p) c -> p k c", p=128),
                      writes=WS.all())
                for mt in range(2):
                    b = nps()
                    for k in range(KC):
                        P.op("pe", lambda e, b=b, k=k, mt=mt: e.matmul(
                            PS[b][:, :], lhsT=MEMN.ap[:, k, mt * 128:(mt + 1) * 128], rhs=WS.ap[:, k, :],
                            start=(k == 0), stop=(k == KC - 1)),
                            reads=MEMN.all() + WS.all(), writes=[aPS[b]])
                    P.op("act", lambda e, b=b, mt=mt, n2=n2: e.activation(
                        out=VX.ap[:, mt, n2 * TT:(n2 + 1) * TT], in_=PS[b][:, :], func=AF.Copy),
                        reads=[aPS[b]], writes=VX.all())
            for m in range(KC):
                s = load_w_block(dr["xa_wq"], l, 0, KC, m * 128)
                for t in range(NT):
                    rf, ra = xn_rhs(t)
                    b = nps()
                    mm_group(b, s, KC, rf, ra)
                    P.op("act", lambda e, b=b, m=m, t=t: e.activation(
                        out=QX.ap[:, m, ts(t)], in_=PS[b][:, :], func=AF.Copy, scale=1.0 / 16.0),
                        reads=[aPS[b]], writes=QX.at(m, e0=t * TT, e1=(t + 1) * TT))
            stt = {}

            def st_scores(u):
                tok = slice(u * 128, (u + 1) * 128)
                qa = QX.at(0, KC, u * 128, (u + 1) * 128)
                bs = [nps(), nps()]
                for h in range(4):
                    for dc in range(2):
                        P.op("pe", lambda e, h=h, dc=dc, bs=bs, tok=tok: e.matmul(
                            PS[bs[h // 2]][:, (h % 2) * NMEM:(h % 2 + 1) * NMEM], lhsT=QX.ap[:, h * 2 + dc, tok],
                            rhs=KTX.ap[:, h * 2 + dc, :], start=(dc == 0), stop=(dc == 1)),
                            reads=qa + KTX.all(), writes=[aPS[bs[h // 2]]])
                stt[u] = {"bs": bs}

            def st_softmax(u):
                bs = stt[u]["bs"]
                z = nxt("sm2", 2)
                for hp in range(2):
                    P.op("dve", lambda e, hp=hp, z=z, bs=bs: e.reduce_max(
                        out=SM2[:, z, hp * 2:hp * 2 + 2], in_=PS[bs[hp]][:, :].rearrange("p (a b) -> p a b", a=2),
                        axis=AX.X), reads=[aPS[bs[hp]]], writes=[aSM2[z]])
                P.op("dve", lambda e, z=z: e.tensor_scalar(out=SM2[:, z, 0:4], in0=SM2[:, z, 0:4], scalar1=-1.0,
                                                          scalar2=None, op0=ALU.mult),
                     reads=[aSM2[z]], writes=[aSM2[z]])
                pb = nxt("pb", 2)
                for h in range(4):
                    P.op("act", lambda e, h=h, z=z, pb=pb, bs=bs: e.activation(
                        out=PB[pb].ap[:, h, :], in_=PS[bs[h // 2]][:, (h % 2) * NMEM:(h % 2 + 1) * NMEM],
                        func=AF.Exp, bias=SM2[:, z, h:h + 1], scale=1.0, accum_out=SM2[:, z, 4 + h:5 + h]),
                        reads=[aPS[bs[h // 2]], aSM2[z]], writes=PB[pb].all() + [aSM2[z]])
                P.op("dve", lambda e, z=z: e.reciprocal(out=SM2[:, z, 8:12], in_=SM2[:, z, 4:8]),
                     reads=[aSM2[z]], writes=[aSM2[z]])
                for h in range(4):
                    P.op("dve", lambda e, h=h, z=z, pb=pb: e.tensor_scalar(
                        out=PB[pb].ap[:, h, :], in0=PB[pb].ap[:, h, :], scalar1=SM2[:, z, 8 + h:9 + h], scalar2=None,
                        op0=ALU.mult), reads=PB[pb].all() + [aSM2[z]], writes=PB[pb].all())
                stt[u]["pb"] = pb

            def st_transpose(u):
                pb = stt[u]["pb"]
                bt = nps()
                ptv = PS[bt][:, :].bitcast(BF16).rearrange("p (a b) -> p a b", b=128)
                for h in range(4):
                    for mt in range(2):
                        P.op("pe", lambda e, h=h, mt=mt, pb=pb, ptv=ptv: e.transpose(
                            out=ptv[:, h * 2 + mt, :], in_=PB[pb].ap[:, h, mt * 128:(mt + 1) * 128], identity=IDENT[:, :]),
                            reads=PB[pb].all() + aC, writes=[aPS[bt]])
                px = nxt("ptx", 2)
                P.op("act", lambda e, px=px, ptv=ptv: e.activation(out=PTX[px].ap[:, :, :], in_=ptv[:, 0:8, :], func=AF.Copy),
                     reads=[aPS[bt]], writes=PTX[px].all())
                stt[u]["px"] = px

            def st_pv(u):
                tok = slice(u * 128, (u + 1) * 128)
                px = stt[u]["px"]
                bo = [nps(), nps()]
                for h in range(4):
                    for dc in range(2):
                        idx = h * 2 + dc
                        for mt in range(2):
                            P.op("pe", lambda e, h=h, dc=dc, mt=mt, idx=idx, px=px, bo=bo: e.matmul(
                                PS[bo[idx // 4]][:, (idx % 4) * 128:(idx % 4 + 1) * 128],
                                lhsT=VX.ap[:, mt, h * 256 + dc * 128:h * 256 + (dc + 1) * 128],
                                rhs=PTX[px].ap[:, h * 2 + mt, :], start=(mt == 0), stop=(mt == 1)),
                                reads=VX.all() + PTX[px].all(), writes=[aPS[bo[idx // 4]]])
                for hf in range(2):
                    P.op("act", lambda e, hf=hf, bo=bo, tok=tok: e.activation(
                        out=QX.ap[:, hf * 4:(hf + 1) * 4, tok], in_=PS[bo[hf]][:, :].rearrange("p (a b) -> p a b", a=4),
                        func=AF.Copy), reads=[aPS[bo[hf]]], writes=QX.at(hf * 4, hf * 4 + 4, u * 128, (u + 1) * 128))
                del stt[u]

            for step in range(NCH + 2):
                if step < NCH:
                    st_scores(step)
                    st_softmax(step)
                if 0 <= step - 1 < NCH:
                    st_transpose(step - 1)
                if 0 <= step - 2 < NCH:
                    st_pv(step - 2)
            for m in range(KC):
                s = load_w_block(dr["xa_wo"], l, 0, KC, m * 128)
                for t in range(NT):
                    b = nps()
                    mm_group(b, s, KC, lambda k, t=t: QX.ap[:, k, ts(t)],
                             lambda k, t=t: QX.at(k, e0=t * TT, e1=(t + 1) * TT))
                    x_add(b, m, t, 1.0)

        def fourier(l, g):
            for gg in range(4):
                s = load_w_block(dr["mix_w_in"], l, 0, KC, OFF_HF + gg * 128)
                for t in range(NT):
                    rf, ra = xn_rhs(t)
                    b = nps()
                    mm_group(b, s, KC, rf, ra)
                    P.op("act", lambda e, b=b, gg=gg, t=t: e.activation(out=HFY.ap[:, gg, ts(t)], in_=PS[b][:, :], func=AF.Copy),
                         reads=[aPS[b]], writes=HFY.at(gg, e0=t * TT, e1=(t + 1) * TT))
            for j in range(NCH):
                bb = [nps(), nps()]
                for gg in range(4):
                    P.op("pe", lambda e, gg=gg, j=j, bb=bb: e.matmul(
                        PS[bb[gg // 2]][:, (gg % 2) * 256:(gg % 2 + 1) * 256], lhsT=HFY.ap[:, gg, j * 128:(j + 1) * 128],
                        rhs=FC[:, :], start=True, stop=True),
                        reads=HFY.at(gg, e0=j * 128, e1=(j + 1) * 128) + aC, writes=[aPS[bb[gg // 2]]])
                q = nxt("pst", 2)
                for hf in range(2):
                    P.op("act" if hf == 0 else "dve",
                         (lambda e, hf=hf, q=q, bb=bb: e.activation(out=PST[q].ap[:, hf * TT:(hf + 1) * TT], in_=PS[bb[hf]][:, :], func=AF.Copy))
                         if hf == 0 else
                         (lambda e, hf=hf, q=q, bb=bb: e.tensor_copy(out=PST[q].ap[:, hf * TT:(hf + 1) * TT], in_=PS[bb[hf]][:, :])),
                         reads=[aPS[bb[hf]]], writes=PST[q].all())
                P.dma("sp", PSCR.ap()[j * 128:(j + 1) * 128, :], PST[q].ap[:, :], reads=PST[q].all(), writes=[aPSCR[j]])
            if g == "s":
                P.allgather(PGATH.ap(), PSCR.ap(), reads=aPSCR, writes=aPG)
                src, src_atoms, npiece, E = PGATH.ap(), aPG, SSAMP // 512, dr["E_s"]
            else:
                src, src_atoms, npiece, E = PSCR.ap(), aPSCR, T // 512, dr["E_p"]
            srcv = src.rearrange("(j p) c -> p j c", p=128)
            for krp in range(2):
                Evs = [E[krp * 2 + kk].rearrange("(j p) s k -> p j s k", p=128) for kk in range(2)]
                acc = [nps(hold=True) for _ in range(8)]
                for pc in range(npiece):
                    sl = nxt("fpc", 3)
                    se = nxt("fec", 2)
                    P.dma("sp", PPC[sl].ap[:, :, :], srcv[:, pc * 4:(pc + 1) * 4, :], reads=src_atoms, writes=PPC[sl].all())
                    for kk in range(2):
                        P.dma("sp", EPC[se].ap[:, kk * 8:(kk + 1) * 8, :].rearrange("p (j s) k -> p j s k", s=2),
                              Evs[kk][:, pc * 4:(pc + 1) * 4, :, :], writes=EPC[se].at(kk * 8, (kk + 1) * 8))
                    for jj in range(4):
                        for gg in range(4):
                            for ri in range(2):
                                first = (pc == 0 and jj == 0 and ri == 0)
                                last = (pc == npiece - 1 and jj == 3 and ri == 1)
                                for kk in range(2):
                                    P.op("pe", lambda e, sl=sl, se=se, jj=jj, gg=gg, ri=ri, kk=kk, first=first, last=last, acc=acc: e.matmul(
                                        PS[acc[kk * 4 + gg]][:, :],
                                        lhsT=PPC[sl].ap[:, jj, gg * 256 + ri * 128:gg * 256 + (ri + 1) * 128],
                                        rhs=EPC[se].ap[:, kk * 8 + jj * 2 + ri, :], start=first, stop=last),
                                        reads=PPC[sl].all() + EPC[se].at(kk * 8, (kk + 1) * 8), writes=[aPS[acc[kk * 4 + gg]]])
                for kk in range(2):
                    kr = krp * 2 + kk
                    for gg in range(4):
                        b_ = acc[kk * 4 + gg]
                        P.op("act" if gg % 2 == 0 else "dve",
                             (lambda e, gg=gg, kr=kr, b_=b_: e.activation(out=HFY.ap[:, gg, ts(kr)], in_=PS[b_][:, :], func=AF.Copy))
                             if gg % 2 == 0 else
                             (lambda e, gg=gg, kr=kr, b_=b_: e.tensor_copy(out=HFY.ap[:, gg, ts(kr)], in_=PS[b_][:, :])),
                             reads=[aPS[b_]], writes=HFY.at(gg, e0=kr * TT, e1=(kr + 1) * TT))
                for b_ in acc:
                    held.discard(b_)

        def proj_rot(l, g, h, off, dst):
            s1 = load_w_block(dr["mix_w_in"], l, 0, KC, off + h * 128)
            s2 = load_w_block(dr["mix_w_in"], l, 0, KC, off + h * 128, swap=True)
            rot = dr["rot_" + g]
            for t in range(NT):
                rb = nxt("rot", 1)
                P.dma("sp", ROTB[:, rb, :, :], rot[:, :, ts(t)].rearrange("s p t -> p s t"), writes=[aROT[rb]])
                rf, ra = xn_rhs(t)
                b1 = nps()
                mm_group(b1, s1, KC, rf, ra)
                b2 = nps()
                mm_group(b2, s2, KC, rf, ra)
                m1 = nxt("tmp", NTMP)
                P.op("dve", lambda e, b1=b1, m1=m1, rb=rb: e.tensor_tensor(
                    out=TMP[:, m1, :], in0=PS[b1][:, :], in1=ROTB[:, rb, 0, :], op=ALU.mult),
                    reads=[aPS[b1], aROT[rb]], writes=[aTMP[m1]])
                m2 = nxt("tmp", NTMP)
                P.op("dve", lambda e, b2=b2, m2=m2, rb=rb: e.tensor_tensor(
                    out=TMP[:, m2, :], in0=PS[b2][:, :], in1=ROTB[:, rb, 1, :], op=ALU.mult),
                    reads=[aPS[b2], aROT[rb]], writes=[aTMP[m2]])
                P.op("dve", lambda e, m1=m1, m2=m2, t=t: e.tensor_tensor(
                    out=dst.ap[:, ts(t)], in0=TMP[:, m1, :], in1=TMP[:, m2, :], op=ALU.add),
                    reads=[aTMP[m1], aTMP[m2]], writes=dst.at(e0=t * TT, e1=(t + 1) * TT))

        def retention_head(l, g, h):
            cf, cb = l * 4 + h, 16 + l * 4 + h
            lgf, lgb = LG[:, cf:cf + 1], LG[:, cb:cb + 1]
            nlgb = NLG[:, cb:cb + 1]
            proj_rot(l, g, h, OFF_K, KTH)
            P.dma("pool", WV.ap[:, :, :],
                  dr["mix_w_in"][l, :, OFF_V + h * 256:OFF_V + (h + 1) * 256].rearrange("(k p) c -> p k c", p=128),
                  writes=WV.all())
            for j in range(NCH):
                b = nps()
                for k in range(KC):
                    P.op("pe", lambda e, b=b, k=k, j=j: e.matmul(
                        PS[b][:, 0:256], lhsT=XN[:, k, j * 128:(j + 1) * 128], rhs=WV.ap[:, k, :],
                        start=(k == 0), stop=(k == KC - 1)),
                        reads=[aXN[j // 4]] + WV.all(), writes=[aPS[b]])
                P.op("act", lambda e, b=b, j=j: e.activation(out=VH.ap[:, j, :], in_=PS[b][:, 0:256], func=AF.Copy),
                     reads=[aPS[b]], writes=VH.at(j))
            P.op("act", lambda e: e.activation(out=SM[:, 0:16], in_=TIDX[:, :], func=AF.Exp, scale=lgf),
                 reads=aC + aLG, writes=aSM)
            P.op("act", lambda e: e.activation(out=SM[:, 16:32], in_=TIDX[:, :], func=AF.Exp, scale=lgb),
                 reads=aC + aLG, writes=aSM)
            P.op("dve", lambda e: e.tensor_scalar(out=SM[:, 32:33], in0=lgf, scalar1=128.0, scalar2=LN_S,
                                                  op0=ALU.mult, op1=ALU.add), reads=aLG, writes=aSM)
            P.op("dve", lambda e: e.tensor_scalar(out=SM[:, 33:34], in0=lgb, scalar1=512.0, scalar2=LN_S,
                                                  op0=ALU.mult, op1=ALU.add), reads=aLG, writes=aSM)
            P.op("act", lambda e: e.activation(out=TAB.ap[:, 0, :], in_=D0[:, :], func=AF.Exp, scale=lgf, bias=SM[:, 32:33]),
                 reads=aC + aLG + aSM, writes=TAB.at(0))
            P.op("act", lambda e: e.activation(out=TAB.ap[:, 1, :], in_=D0[:, :], func=AF.Exp, scale=nlgb, bias=SM[:, 33:34]),
                 reads=aC + aLG + aSM, writes=TAB.at(1))
            for r in range(4):
                m1, m2 = nxt("tmp", NTMP), nxt("tmp", NTMP)
                P.op("dve", lambda e, r=r, m1=m1: e.tensor_scalar(
                    out=TMP[:, m1, :], in0=D0[:, :], scalar1=-128.0 * r, scalar2=0.0, op0=ALU.add, op1=ALU.max),
                    reads=aC, writes=[aTMP[m1]])
                P.op("dve", lambda e, m1=m1: e.tensor_scalar(
                    out=TMP[:, m1, :], in0=TMP[:, m1, :], scalar1=lgf, scalar2=None, op0=ALU.mult),
                    reads=[aTMP[m1]] + aLG, writes=[aTMP[m1]])
                P.op("dve", lambda e, r=r, m2=m2: e.tensor_scalar(
                    out=TMP[:, m2, :], in0=D0[:, :], scalar1=-128.0 * r, scalar2=0.0, op0=ALU.add, op1=ALU.min),
                    reads=aC, writes=[aTMP[m2]])
                P.op("dve", lambda e, m1=m1, m2=m2: e.scalar_tensor_tensor(
                    out=TMP[:, m1, :], in0=TMP[:, m2, :], scalar=nlgb, in1=TMP[:, m1, :], op0=ALU.mult, op1=ALU.add),
                    reads=[aTMP[m1], aTMP[m2]] + aLG, writes=[aTMP[m1]])
                P.op("act", lambda e, r=r, m1=m1: e.activation(out=TAB.ap[:, 2 + r, :], in_=TMP[:, m1, :], func=AF.Exp,
                                                             bias=LN_S), reads=[aTMP[m1]], writes=TAB.at(2 + r))
            if g == "s":
                P.op("act", lambda e: e.activation(out=SM[:, 64:80], in_=ZIDX[:, 0, :], func=AF.Exp, scale=lgf),
                     reads=aC + aLG, writes=aSM)
                P.op("act", lambda e: e.activation(out=SM[:, 80:96], in_=ZIDX[:, 1, :], func=AF.Exp, scale=lgb),
                     reads=aC + aLG, writes=aSM)
                P.op("act", lambda e: e.activation(out=SM[:, 48:56], in_=CIDX[:, 0, :], func=AF.Exp, scale=lgf),
                     reads=aC + aLG, writes=aSM)
                P.op("act", lambda e: e.activation(out=SM[:, 56:64], in_=CIDX[:, 2, :], func=AF.Exp, scale=lgb),
                     reads=aC + aLG, writes=aSM)
                P.op("dve", lambda e: e.tensor_tensor(out=SM[:, 48:56], in0=SM[:, 48:56], in1=CIDX[:, 1, :], op=ALU.mult),
                     reads=aSM + aC, writes=aSM)
                P.op("dve", lambda e: e.tensor_tensor(out=SM[:, 56:64], in0=SM[:, 56:64], in1=CIDX[:, 3, :], op=ALU.mult),
                     reads=aSM + aC, writes=aSM)
                P.op("dve", lambda e: e.tensor_scalar(out=SM[:, 34:38], in0=TIDX[:, 0:16:4], scalar1=lgf, scalar2=None,
                                                      op0=ALU.mult), reads=aC + aLG, writes=aSM)
                P.op("dve", lambda e: e.tensor_scalar(out=SM[:, 38:42], in0=TIDX[:, 0:16:4], scalar1=-1.0, scalar2=2049.0,
                                                      op0=ALU.mult, op1=ALU.add), reads=aC, writes=aSM)
                P.op("dve", lambda e: e.tensor_scalar(out=SM[:, 38:42], in0=SM[:, 38:42], scalar1=lgb, scalar2=None,
                                                      op0=ALU.mult), reads=aSM + aLG, writes=aSM)
                bu = nps(hold=True)
                for j in range(NCH):
                    bt = nps()
                    ktv = PS[bt][:, :].bitcast(BF16)
                    P.op("pe", lambda e, j=j, ktv=ktv: e.transpose(out=ktv[:, 0:128], in_=KTH.ap[:, j * 128:(j + 1) * 128],
                                                                 identity=IDENT[:, :]),
                         reads=KTH.at(e0=j * 128, e1=(j + 1) * 128) + aC, writes=[aPS[bt]])
                    kb = nxt("kz", 2)
                    P.op("act", lambda e, j=j, kb=kb, ktv=ktv: e.activation(
                        out=KZ.ap[:, kb * 2, :], in_=ktv[:, 0:128], func=AF.Copy, scale=SM[:, 64 + j:65 + j]),
                        reads=[aPS[bt]] + aSM, writes=KZ.all())
                    P.op("dve", lambda e, j=j, kb=kb, ktv=ktv: e.tensor_scalar(
                        out=KZ.ap[:, kb * 2 + 1, :], in0=ktv[:, 0:128], scalar1=SM[:, 80 + j:81 + j], scalar2=None,
                        op0=ALU.mult), reads=[aPS[bt]] + aSM, writes=KZ.all())
                    for dd in range(2):
                        P.op("pe", lambda e, j=j, kb=kb, dd=dd, bu=bu: e.matmul(
                            PS[bu][:, dd * 256:(dd + 1) * 256], lhsT=KZ.ap[:, kb * 2 + dd, :], rhs=VH.ap[:, j, :],
                            start=(j == 0), stop=(j == NCH - 1)),
                            reads=KZ.all() + VH.at(j), writes=[aPS[bu]])
                held.discard(bu)
                mu = nxt("tmp", NTMP)
                P.op("act", lambda e, mu=mu, bu=bu: e.activation(out=TMP[:, mu, :], in_=PS[bu][:, :], func=AF.Copy),
                     reads=[aPS[bu]], writes=[aTMP[mu]])
                P.dma("sp", UBN.ap(), TMP[:, mu, :], reads=[aTMP[mu]], writes=aUB)
                P.allgather(UGATH.ap(), UBN.ap(), reads=aUB, writes=aUG)
            proj_rot(l, g, h, OFF_Q, QTH)
            if g == "s":
                mr = nxt("tmp", NTMP)
                for r in range(NCORES):
                    ml = nxt("tmp", NTMP)
                    if ml == mr:
                        ml = nxt("tmp", NTMP)
                    P.dma("sp", TMP[:, ml, :], UGATH.ap()[r * 128:(r + 1) * 128, :], reads=aUG, writes=[aTMP[ml]])
                    for dd in range(2):
                        cs = slice(dd * 256, (dd + 1) * 256)
                        col = 48 + dd * 8 + r
                        if r == 0:
                            P.op("dve", lambda e, ml=ml, mr=mr, cs=cs, col=col: e.tensor_scalar(
                                out=TMP[:, mr, cs], in0=TMP[:, ml, cs], scalar1=SM[:, col:col + 1], scalar2=None,
                                op0=ALU.mult), reads=[aTMP[ml]] + aSM, writes=[aTMP[mr]])
                        else:
                            P.op("dve", lambda e, ml=ml, mr=mr, cs=cs, col=col: e.scalar_tensor_tensor(
                                out=TMP[:, mr, cs], in0=TMP[:, ml, cs], scalar=SM[:, col:col + 1], in1=TMP[:, mr, cs],
                                op0=ALU.mult, op1=ALU.add), reads=[aTMP[ml], aTMP[mr]] + aSM, writes=[aTMP[mr]])
                P.op("act", lambda e, mr=mr: e.activation(out=RINB.ap[:, :], in_=TMP[:, mr, :], func=AF.Copy),
                     reads=[aTMP[mr]], writes=RINB.all())
            for i in range(NT):
                if g == "s":
                    for dd, dst in ((0, QXF), (1, QXB)):
                        mx = nxt("tmp", NTMP)
                        sc = lgf if dd == 0 else nlgb
                        bcol = 34 + dd * 4 + i
                        P.op("act", lambda e, mx=mx, sc=sc, bcol=bcol: e.activation(
                            out=TMP[:, mx, :], in_=XIDX[:, :], func=AF.Exp, scale=sc, bias=SM[:, bcol:bcol + 1]),
                            reads=aC + aLG + aSM, writes=[aTMP[mx]])
                        P.op("dve", lambda e, mx=mx, dst=dst, i=i: e.tensor_tensor(
                            out=dst.ap[:, :], in0=QTH.ap[:, ts(i)], in1=TMP[:, mx, :], op=ALU.mult),
                            reads=QTH.at(e0=i * TT, e1=(i + 1) * TT) + [aTMP[mx]], writes=dst.all())
                by = [nps(hold=True), nps(hold=True)]
                sbank = {}

                def emit_s(j, i=i):
                    bs = nps()
                    sbank[j] = bs
                    P.op("pe", lambda e, bs=bs, j=j, i=i: e.matmul(
                        PS[bs][:, :], lhsT=KTH.ap[:, j * 128:(j + 1) * 128], rhs=QTH.ap[:, ts(i)], start=True, stop=True),
                        reads=KTH.at(e0=j * 128, e1=(j + 1) * 128) + QTH.at(e0=i * TT, e1=(i + 1) * TT), writes=[aPS[bs]])

                LOOK = 2
                for j in range(min(LOOK, NCH)):
                    emit_s(j)
                for j in range(NCH):
                    bs = sbank.pop(j)
                    ab = nxt("atb", 3)
                    if j <= 4 * i - 1:
                        tcol, tb = 4 * i - j - 1, 0
                    elif j >= 4 * i + 4:
                        tcol, tb = 16 + j - 4 * i - 4, 1
                    else:
                        tcol, tb = None, 2 + (j - 4 * i)
                    if tcol is None:
                        P.op("dve", lambda e, bs=bs, ab=ab, tb=tb: e.tensor_tensor(
                            out=ATB[ab].ap[:, :], in0=PS[bs][:, :], in1=TAB.ap[:, tb, :], op=ALU.mult),
                            reads=[aPS[bs]] + TAB.at(tb), writes=ATB[ab].all())
                    else:
                        P.op("dve", lambda e, bs=bs, ab=ab, tb=tb, tcol=tcol: e.scalar_tensor_tensor(
                            out=ATB[ab].ap[:, :], in0=PS[bs][:, :], scalar=SM[:, tcol:tcol + 1], in1=TAB.ap[:, tb, :],
                            op0=ALU.mult, op1=ALU.mult),
                            reads=[aPS[bs]] + TAB.at(tb) + aSM, writes=ATB[ab].all())
                    if j + LOOK < NCH:
                        emit_s(j + LOOK)
                    for ee in range(2):
                        last = (j == NCH - 1) and g != "s"
                        P.op("pe", lambda e, ee=ee, j=j, ab=ab, by=by, last=last: e.matmul(
                            PS[by[ee]][:, :], lhsT=VH.ap[:, j, ee * 128:(ee + 1) * 128], rhs=ATB[ab].ap[:, :],
                            start=(j == 0), stop=last),
                            reads=VH.at(j) + ATB[ab].all(), writes=[aPS[by[ee]]])
                if g == "s":
                    for ee in range(2):
                        for dd, srcq in ((0, QXF), (1, QXB)):
                            P.op("pe", lambda e, ee=ee, dd=dd, srcq=srcq, by=by: e.matmul(
                                PS[by[ee]][:, :], lhsT=RINB.ap[:, dd * 256 + ee * 128:dd * 256 + (ee + 1) * 128],
                                rhs=srcq.ap[:, :], start=False, stop=(dd == 1)),
                                reads=RINB.all() + srcq.all(), writes=[aPS[by[ee]]])
                held.discard(by[0])
                held.discard(by[1])
                for ee in range(2):
                    P.op("act", lambda e, ee=ee, by=by: e.activation(out=SQ2.ap[:, ee, :], in_=PS[by[ee]][:, :], func=AF.Square),
                         reads=[aPS[by[ee]]], writes=SQ2.all())
                bn = nps()
                for ee in range(2):
                    P.op("pe", lambda e, ee=ee, bn=bn: e.matmul(PS[bn][:, :], lhsT=ONES[:, :], rhs=SQ2.ap[:, ee, :],
                                                              start=(ee == 0), stop=(ee == 1)),
                         reads=SQ2.all() + aC, writes=[aPS[bn]])
                r = nxt("rs", 2)
                P.op("act", lambda e, bn=bn, r=r: e.activation(out=RS[:, r, :], in_=PS[bn][:, :], func=AF.Ln,
                                                             scale=1.0 / 256.0, bias=EPS),
                     reads=[aPS[bn]], writes=[aRS[r]])
                P.op("act", lambda e, r=r: e.activation(out=RS[:, r, :], in_=RS[:, r, :], func=AF.Exp, scale=-0.5),
                     reads=[aRS[r]], writes=[aRS[r]])
                for ee in range(2):
                    mo = nxt("tmp", NTMP)
                    row = h * 2 + ee
                    P.op("dve", lambda e, ee=ee, by=by, mo=mo, r=r: e.tensor_tensor(
                        out=TMP[:, mo, :], in0=PS[by[ee]][:, :], in1=RS[:, r, :], op=ALU.mult),
                        reads=[aPS[by[ee]], aRS[r]], writes=[aTMP[mo]])
                    P.op("dve", lambda e, mo=mo, row=row, i=i: e.tensor_tensor(
                        out=RT.ap[:, row, ts(i)], in0=TMP[:, mo, :], in1=RT.ap[:, row, ts(i)], op=ALU.mult),
                        reads=[aTMP[mo]] + RT.at(row, e0=i * TT, e1=(i + 1) * TT),
                        writes=RT.at(row, e0=i * TT, e1=(i + 1) * TT))

        def mix_block(l, g):
            rmsnorm_to_xn(l * 5 + G_MIX)
            if do_fourier:
                fourier(l, g)
            for m in range(KC):
                s = load_w_block(dr["mix_w_in"], l, 0, KC, OFF_G + m * 128)
                for t in range(NT):
                    rf, ra = xn_rhs(t)
                    b = nps()
                    mm_group(b, s, KC, rf, ra)
                    P.op("act", lambda e, b=b, m=m, t=t: e.activation(out=RT.ap[:, m, ts(t)], in_=PS[b][:, :], func=AF.Silu),
                         reads=[aPS[b]], writes=RT.at(m, e0=t * TT, e1=(t + 1) * TT))
            if do_ret:
                for h in range(4):
                    retention_head(l, g, h)
            for m in range(KC):
                sa = load_w_block(dr["mix_w_in"], l, 0, KC, OFF_GA + m * 128)
                sbb = load_w_block(dr["mix_w_in"], l, 0, KC, OFF_GB + m * 128)
                sf = load_w_block(dr["fourier_w"], l, 0, 4, m * 128)
                sr = load_w_block(dr["ret_w_out"], l, 0, KC, m * 128)
                for t in range(NT):
                    rf, ra = xn_rhs(t)
                    ba = nps()
                    mm_group(ba, sa, KC, rf, ra)
                    bb = nps()
                    mm_group(bb, sbb, KC, rf, ra)
                    bf = nps()
                    mm_group(bf, sf, 4, lambda k, t=t: HFY.ap[:, k, ts(t)], lambda k, t=t: HFY.at(k, e0=t * TT, e1=(t + 1) * TT))
                    br = nps()
                    mm_group(br, sr, KC, lambda k, t=t: RT.ap[:, k, ts(t)], lambda k, t=t: RT.at(k, e0=t * TT, e1=(t + 1) * TT))
                    m1, m2 = nxt("tmp", NTMP), nxt("tmp", NTMP)
                    P.op("act", lambda e, ba=ba, m1=m1: e.activation(out=TMP[:, m1, :], in_=PS[ba][:, :], func=AF.Sigmoid),
                         reads=[aPS[ba]], writes=[aTMP[m1]])
                    P.op("act", lambda e, bb=bb, m2=m2: e.activation(out=TMP[:, m2, :], in_=PS[bb][:, :], func=AF.Sigmoid),
                         reads=[aPS[bb]], writes=[aTMP[m2]])
                    P.op("dve", lambda e, bf=bf, m1=m1: e.tensor_tensor(out=TMP[:, m1, :], in0=PS[bf][:, :], in1=TMP[:, m1, :],
                                                                       op=ALU.mult), reads=[aPS[bf], aTMP[m1]], writes=[aTMP[m1]])
                    P.op("dve", lambda e, br=br, m2=m2: e.tensor_tensor(out=TMP[:, m2, :], in0=PS[br][:, :], in1=TMP[:, m2, :],
                                                                       op=ALU.mult), reads=[aPS[br], aTMP[m2]], writes=[aTMP[m2]])
                    P.op("dve", lambda e, m1=m1, m2=m2, m=m, t=t: e.tensor_tensor(
                        out=MERGED.ap[:, m, ts(t)], in0=TMP[:, m1, :], in1=TMP[:, m2, :], op=ALU.add),
                        reads=[aTMP[m1], aTMP[m2]], writes=MERGED.at(m, e0=t * TT, e1=(t + 1) * TT))
            for m in range(KC):
                s = load_w_block(dr["mix_w_out"], l, 0, KC, m * 128)
                for t in range(NT):
                    b = nps()
                    mm_group(b, s, KC, lambda k, t=t: MERGED.ap[:, k, ts(t)],
                             lambda k, t=t: MERGED.at(k, e0=t * TT, e1=(t + 1) * TT))
                    x_add(b, m, t, 1.0)

        for g in groups:
            xin = dr["xT_" + g].rearrange("(k p) t -> p k t", p=128)
            for t in range(NT):
                P.dma("sp", X[:, :, ts(t)], xin[:, :, ts(t)], writes=[aX[k][t] for k in range(KC)])
            for l in range(n_layers):
                if do_ffn:
                    ffn(l, dr["ffn1_w_in"], dr["ffn1_w_out"], G_FFN1)
                if do_mix:
                    mix_block(l, g)
                if do_xa:
                    xa_block(l, g)
                if do_ffn:
                    ffn(l, dr["ffn2_w_in"], dr["ffn2_w_out"], G_FFN2)
            yout = yT[g].rearrange("(k p) t -> p k t", p=128)
            for t in range(NT):
                xa_ = [aX[k][t] for k in range(KC)]
                r = norm_stats(lambda t=t: X[:, :, ts(t)], TT, xa_)
                for k in range(KC):
                    col = 20 * KC + k
                    o = nxt("tmp", NTMP)
                    P.op("dve", lambda e, k=k, t=t, r=r, col=col, o=o: e.scalar_tensor_tensor(
                        out=TMP[:, o, :], in0=X[:, k, ts(t)], scalar=GA[:, col:col + 1], in1=RS[:, r, :],
                        op0=ALU.mult, op1=ALU.mult),
                        reads=[aX[k][t], aRS[r]] + aC, writes=[aTMP[o]])
                    P.dma("sp", yout[:, k, ts(t)], TMP[:, o, :], reads=[aTMP[o]], writes=aOUT)
        P.finish(aOUT)

        @block.tensor
        def _(e):
            P.replay("pe", e)

        @block.scalar
        def _(e):
            P.replay("act", e)

        @block.vector
        def _(e):
            P.replay("dve", e)

        @block.gpsimd
        def _(e):
            P.replay("pool", e)

        @block.sync
        def _(e):
            P.replay("sp", e)

    return nc


_CONST_CACHE = {}


def _rot_table(pos0):
    half = 64
    inv = 1.0 / (10000.0 ** (np.arange(half, dtype=np.float64) * 2.0 / 128.0))
    pos = pos0 + np.arange(T, dtype=np.float64)
    ang = pos[None, :] * inv[:, None]
    c = np.cos(ang)
    s = np.sin(ang)
    C = np.concatenate([c, c], 0)
    S = np.concatenate([-s, s], 0)
    return np.stack([C, S], 0).astype(np.float32)


def _dft_table(S, k0):
    import ml_dtypes
    n = np.arange(S, dtype=np.int64)[:, None]
    out = np.empty((4, S, 2, TT), dtype=ml_dtypes.bfloat16)
    sc = 1.0 / math.sqrt(S)
    for kr in range(4):
        k = (k0 + kr * TT + np.arange(TT, dtype=np.int64))[None, :]
        ang = ((n * k) % S).astype(np.float64) * (2.0 * math.pi / S)
        out[kr, :, 0, :] = (np.cos(ang) * sc).astype(ml_dtypes.bfloat16)
        out[kr, :, 1, :] = (np.sin(ang) * sc).astype(ml_dtypes.bfloat16)
    return out


def _host_consts():
    import ml_dtypes
    if "shared" in _CONST_CACHE:
        return _CONST_CACHE["shared"]
    bf = ml_dtypes.bfloat16
    c = np.arange(128, dtype=np.float64)
    ang = 2.0 * math.pi * np.outer(c, c) / 128.0
    fc = np.concatenate([np.cos(ang), -np.sin(ang)], 1) / math.sqrt(128.0)
    m = np.arange(128, dtype=np.float64)[:, None]
    cc = np.arange(TT, dtype=np.float64)[None, :]
    j = np.arange(16, dtype=np.float64)[None, :]
    p = np.arange(128, dtype=np.float64)[:, None]
    out = {
        "ones_bf": np.ones((128, 128), dtype=bf),
        "ident_bf": np.eye(128, dtype=np.float32).astype(bf),
        "fc_bf": fc.astype(np.float32).astype(bf),
        "d0": (cc - m).astype(np.float32) * np.ones((128, 1), np.float32),
        "xidx": np.broadcast_to(cc + 1.0, (128, TT)).astype(np.float32).copy(),
        "tidx": np.broadcast_to(128.0 * j, (128, 16)).astype(np.float32).copy(),
        "zidx": np.stack([2047.0 - (128.0 * j + p), 128.0 * j + p], 1).astype(np.float32),
        "rot_p": _rot_table(0.0),
        "E_p": _dft_table(T, 0),
    }
    _CONST_CACHE["shared"] = out
    return out


def _core_consts(c):
    key = ("core", c)
    if key in _CONST_CACHE:
        return _CONST_CACHE[key]
    r = np.arange(NCORES, dtype=np.float64)
    jf = np.where(r < c, 2048.0 * (c - 1 - r), 0.0)
    mf = (r < c).astype(np.float64) * (128.0 ** -0.5)
    jb = np.where(r > c, 2048.0 * (r - c - 1), 0.0)
    mb = (r > c).astype(np.float64) * (128.0 ** -0.5)
    cidx = np.broadcast_to(np.stack([jf, mf, jb, mb], 0)[None], (128, 4, NCORES)).astype(np.float32).copy()
    out = {"cidx": cidx, "rot_s": _rot_table(float(c * T)), "E_s": _dft_table(SSAMP, c * T)}
    _CONST_CACHE[key] = out
    return out


def kernel(**inp):
    f32 = np.float32
    xp = np.asarray(inp["x_prompt"], f32)
    xs = np.asarray(inp["x_sample"], f32)
    mp = np.asarray(inp["mem_prompt"], f32)
    ms = np.asarray(inp["mem_sample"], f32)
    gl = []
    for l in range(DEPTH):
        for n in ["ffn1_norm", "mix_norm", "xa_norm", "mem_norm", "ffn2_norm"]:
            gl.append(np.asarray(inp[n], f32)[l])
    gl.append(np.asarray(inp["final_norm"], f32))
    gains = np.ascontiguousarray(np.stack(gl, 0).reshape(21, KC, 128).transpose(2, 0, 1).reshape(128, 21 * KC))
    decay = np.concatenate([np.asarray(inp["ret_decay_fwd"], f32).reshape(-1),
                            np.asarray(inp["ret_decay_bwd"], f32).reshape(-1)])[None, :]
    shared = {w: np.ascontiguousarray(np.asarray(inp[w], f32)) for w in WNAMES}
    shared["gains"] = gains
    shared["decay"] = np.ascontiguousarray(decay)
    shared.update(_host_consts())
    in_maps = []
    for c in range(NCORES):
        m = dict(shared)
        m.update(_core_consts(c))
        m["xT_p"] = np.ascontiguousarray(xp[c].T)
        m["xT_s"] = np.ascontiguousarray(xs[0, c * T:(c + 1) * T].T)
        m["memT_p"] = np.ascontiguousarray(mp[c].T)
        m["memT_s"] = np.ascontiguousarray(ms[0].T)
        in_maps.append(m)
    nc = build_nc()
    res = run_bass_kernel_spmd(nc, in_maps, core_ids=list(range(NCORES)))
    yp = np.stack([res.results[c]["yT_p"].T for c in range(NCORES)], 0).astype(f32)
    ys = np.concatenate([res.results[c]["yT_s"].T for c in range(NCORES)], 0)[None].astype(f32)
    return (np.ascontiguousarray(yp), np.ascontiguousarray(ys))
```

```python
import contextlib
import math
import numpy as np
import concourse.bass as bass
import concourse.mybir as mybir
from concourse.bass_utils import run_bass_kernel_spmd

F32 = mybir.dt.float32
BF16 = mybir.dt.bfloat16
AF = mybir.ActivationFunctionType
ALU = mybir.AluOpType
AX = mybir.AxisListType

D = 1024
KC = 8
T = 2048
TT = 512
NT = T // TT
NCH = T // 128
DEPTH = 4
DFF = 2816
NF = DFF // 128
NH = NF // 2
NMEM = 256
EPS = 1e-6
NCORES = 8
SSAMP = NCORES * T
LN_S = math.log(128.0 ** -0.5)
OFF_HF, OFF_Q, OFF_K, OFF_V, OFF_G, OFF_GA, OFF_GB = 0, 512, 1024, 1536, 2560, 3584, 4608

WNAMES = ["ffn1_w_in", "ffn1_w_out", "mix_w_in", "fourier_w", "ret_w_out", "mix_w_out",
          "xa_wq", "xa_wkv", "xa_wo", "ffn2_w_in", "ffn2_w_out"]
WSHAPES = {"ffn1_w_in": (D, 2 * DFF), "ffn1_w_out": (DFF, D), "mix_w_in": (D, 5632),
           "fourier_w": (512, D), "ret_w_out": (D, D), "mix_w_out": (D, D), "xa_wq": (D, D),
           "xa_wkv": (D, 2 * D), "xa_wo": (D, D), "ffn2_w_in": (D, 2 * DFF), "ffn2_w_out": (DFF, D)}
G_FFN1, G_MIX, G_XA, G_MEM, G_FFN2 = range(5)
AW = 512
NATOM = 80


class Atom:
    __slots__ = ("w", "r")

    def __init__(self):
        self.w = None
        self.r = {}


def atoms(n):
    return [Atom() for _ in range(n)]


class Eng:
    def __init__(self, name, sem, is_pe=False):
        self.name = name
        self.sem = sem
        self.is_pe = is_pe
        self.ops = []
        self.cnt = 0
        self.known = {}


class Prog:
    def __init__(self, nc, sems):
        self.nc = nc
        self.sems = sems
        self.eng = {}
        self.free_sems = list(range(len(sems)))
        for n in ["pe", "act", "dve", "pool", "sp"]:
            self.eng[n] = Eng(n, self.free_sems.pop(0), is_pe=(n == "pe"))
        self.dma_ring = {"sp": [[self.free_sems.pop(0), 0] for _ in range(24)],
                         "pool": [[self.free_sems.pop(0), 0] for _ in range(24)]}
        self.dma_next = {"sp": 0, "pool": 0}
        self.cc_sem = self.free_sems.pop(0)
        self.cc_cnt = 0
        self.n_instr = 0

    def _deps(self, eng, reads, writes):
        deps = {}
        for a in reads:
            if a.w is not None:
                s, v = a.w
                if deps.get(s, 0) < v:
                    deps[s] = v
        for a in writes:
            if a.w is not None:
                s, v = a.w
                if deps.get(s, 0) < v:
                    deps[s] = v
            for s, v in a.r.items():
                if deps.get(s, 0) < v:
                    deps[s] = v
        for s, v in deps.items():
            if s == eng.sem and eng.is_pe:
                continue
            if eng.known.get(s, 0) >= v:
                continue
            eng.known[s] = v
            eng.ops.append(("wait", s, v))

    def op(self, ename, fn, reads=(), writes=(), inc=True):
        eng = self.eng[ename]
        self._deps(eng, reads, writes)
        if inc:
            eng.cnt += 1
            eng.ops.append(("op", fn))
            v = eng.cnt
        else:
            eng.ops.append(("opn", fn))
            v = eng.cnt + 1
        self.n_instr += 1
        for a in reads:
            a.r[eng.sem] = v
        for a in writes:
            a.w = (eng.sem, v)
            a.r = {}

    def dma(self, qname, out_ap, in_ap, reads=(), writes=()):
        q = self.eng[qname]
        ring = self.dma_ring[qname]
        slot = ring[self.dma_next[qname]]
        self.dma_next[qname] = (self.dma_next[qname] + 1) % len(ring)
        s = slot[0]
        self._deps(q, reads, writes)
        if slot[1] > 0 and q.known.get(s, 0) < 16 * slot[1]:
            q.known[s] = 16 * slot[1]
            q.ops.append(("wait", s, 16 * slot[1]))
        slot[1] += 1
        v = 16 * slot[1]
        q.ops.append(("dma", out_ap, in_ap, s))
        self.n_instr += 1
        for a in reads:
            a.r[s] = v
        for a in writes:
            a.w = (s, v)
            a.r = {}

    def allgather(self, out_ap, in_ap, reads, writes):
        q = self.eng["pool"]
        self._deps(q, reads, writes)
        self.cc_cnt += 1
        v = self.cc_cnt
        q.ops.append(("cc", out_ap, in_ap, self.cc_sem))
        for a in reads:
            a.r[self.cc_sem] = v
        for a in writes:
            a.w = (self.cc_sem, v)
            a.r = {}

    def finish(self, out_atoms):
        sp = self.eng["sp"]
        self._deps(sp, out_atoms, ())

    def replay(self, ename, e):
        eng = self.eng[ename]
        sems = self.sems
        for o in eng.ops:
            k = o[0]
            if k == "wait":
                e.wait_ge(sems[o[1]], o[2])
            elif k == "op":
                o[1](e).then_inc(sems[eng.sem], 1)
            elif k == "opn":
                o[1](e)
            elif k == "dma":
                e.dma_start(out=o[1], in_=o[2]).then_inc(sems[o[3]], 16)
            elif k == "cc":
                e.collective_compute("AllGather", mybir.AluOpType.bypass,
                                     replica_groups=[list(range(NCORES))],
                                     ins=[o[2]], outs=[o[1]]).then_inc(sems[o[3]])


class AV:
    def __init__(self, arena, aat, a0, shape):
        self.aat = aat
        self.base = a0 * AW
        self.shape = shape
        n = int(np.prod(shape))
        assert self.base + n <= NATOM * AW, (a0, shape)
        flat = arena[:, self.base:self.base + n]
        if len(shape) == 1:
            self.ap = flat
            self.inner = shape[0]
        else:
            self.ap = flat.rearrange("p (a b) -> p a b", b=shape[1])
            self.inner = shape[1]

    def at(self, r0=0, r1=None, e0=0, e1=None):
        if len(self.shape) == 1:
            r0, r1 = 0, 1
        elif r1 is None:
            r1 = r0 + 1
        if e1 is None:
            e1 = self.inner
        out = []
        seen = set()
        for r in range(r0, r1):
            lo = self.base + r * self.inner + e0
            hi = self.base + r * self.inner + e1 - 1
            for a in range(lo // AW, hi // AW + 1):
                if a not in seen:
                    seen.add(a)
                    out.append(self.aat[a])
        return out

    def all(self):
        if len(self.shape) == 1:
            return self.at()
        return self.at(0, self.shape[0])


def build_nc(n_layers=DEPTH, groups=("p", "s"), do_ffn=True, do_mix=True, do_xa=True,
             do_fourier=True, do_ret=True):
    nc = bass.Bass("TRN2", target_bir_lowering=False)
    dr = {}

    def din(name, shape, dt=F32):
        dr[name] = nc.dram_tensor(name, list(shape), dt, kind="ExternalInput").ap()
        return dr[name]

    for g in "ps":
        din("xT_" + g, [D, T])
        din("memT_" + g, [D, NMEM])
        din("rot_" + g, [2, 128, T])
    for w in WNAMES:
        din(w, [DEPTH, WSHAPES[w][0], WSHAPES[w][1]])
    din("gains", [128, 21 * KC])
    din("decay", [1, 32])
    din("ones_bf", [128, 128], BF16)
    din("ident_bf", [128, 128], BF16)
    din("fc_bf", [128, 256], BF16)
    din("d0", [128, TT])
    din("xidx", [128, TT])
    din("tidx", [128, 16])
    din("zidx", [128, 2, 16])
    din("cidx", [128, 4, 8])
    din("E_p", [4, T, 2, TT], BF16)
    din("E_s", [4, SSAMP, 2, TT], BF16)
    yT = {g: nc.dram_tensor("yT_" + g, [D, T], F32, kind="ExternalOutput").ap() for g in "ps"}
    PSCR = nc.dram_tensor("pscr", [T, 1024], BF16)
    PGATH = nc.dram_tensor("pgath", [SSAMP, 1024], BF16)
    UBN = nc.dram_tensor("ubn", [128, 512], F32)
    UGATH = nc.dram_tensor("ugath", [NCORES * 128, 512], F32)

    with contextlib.ExitStack() as es:
        def sb(name, shape, dt):
            return es.enter_context(nc.sbuf_tensor(name, list(shape), dt))

        X = sb("X", [128, KC, T], F32)
        XN = sb("XN", [128, KC, T], BF16)
        ARENA = sb("ARENA", [128, NATOM * AW], BF16)
        NW = 4
        WR = sb("WR", [128, NW, KC, 128], BF16)
        RS = sb("RS", [128, 2, TT], F32)
        NTMP = 3
        TMP = sb("TMP", [128, NTMP, TT], F32)
        ROTB = sb("ROTB", [128, 1, 2, TT], F32)
        GA = sb("GA", [128, 21 * KC], F32)
        ONES = sb("ONES", [128, 128], BF16)
        IDENT = sb("IDENT", [128, 128], BF16)
        FC = sb("FC", [128, 256], BF16)
        D0 = sb("D0", [128, TT], F32)
        XIDX = sb("XIDX", [128, TT], F32)
        TIDX = sb("TIDX", [128, 16], F32)
        ZIDX = sb("ZIDX", [128, 2, 16], F32)
        CIDX = sb("CIDX", [128, 4, 8], F32)
        DEC = sb("DEC", [128, 32], F32)
        LG = sb("LG", [128, 32], F32)
        NLG = sb("NLG", [128, 32], F32)
        SM = sb("SM", [128, 128], F32)
        SM2 = sb("SM2", [128, 2, 16], F32)
        PS = [es.enter_context(nc.psum_tensor("ps%d" % i, [128, TT], F32)) for i in range(8)]
        sems = [es.enter_context(nc.semaphore("s%d" % i)) for i in range(64)]
        block = es.enter_context(nc.Block())

        P = Prog(nc, sems)
        aAR = atoms(NATOM)
        aX = [[Atom() for _ in range(NT)] for _ in range(KC)]
        aXN = atoms(NT)
        aWR = atoms(NW)
        aRS = atoms(2)
        aTMP = atoms(NTMP)
        aROT = atoms(1)
        aC = atoms(1)
        aLG = atoms(1)
        aSM = atoms(1)
        aSM2 = atoms(2)
        aPS = atoms(8)
        aOUT = atoms(1)
        aPSCR = atoms(NCH)
        aPG = atoms(1)
        aUB = atoms(1)
        aUG = atoms(1)
        st = {}

        def nxt(key, n):
            i = st.get(key, 0)
            st[key] = (i + 1) % n
            return i

        held = set()

        def nps(hold=False):
            while True:
                b = nxt("ps", 8)
                if b not in held:
                    break
            if hold:
                held.add(b)
            return b

        def ts(t):
            return slice(t * TT, (t + 1) * TT)

        def av(a0, shape):
            return AV(ARENA, aAR, a0, shape)

        HID = av(0, (NH, T))
        SQ = av(72, (KC, TT))
        HFY = av(0, (4, T))
        RT = av(16, (KC, T))
        MERGED = av(48, (KC, T))
        PPC = [av(16 + 8 * i, (4, 1024)) for i in range(3)]
        EPC = [av(40 + 8 * i, (8, TT)) for i in range(3)]
        PST = [av(64 + 2 * i, (1024,)) for i in range(2)]
        KTH = av(48, (T,))
        QTH = av(52, (T,))
        VH = av(56, (NCH, 256))
        TAB = av(64, (6, TT))
        ATB = [av(70 + i, (TT,)) for i in range(3)]
        QXF = av(73, (TT,))
        QXB = av(74, (TT,))
        RINB = av(75, (TT,))
        KZ = av(73, (4, 128))
        SQ2 = av(76, (2, TT))
        WV = av(64, (KC, 256))
        QX = av(0, (KC, T))
        KTX = av(32, (KC, NMEM))
        VX = av(36, (2, 1024))
        MEMN = av(40, (KC, NMEM))
        PB = [av(44 + 2 * i, (4, NMEM)) for i in range(2)]
        PTX = [av(48 + 2 * i, (KC, 128)) for i in range(2)]
        WS = av(52, (KC, TT))

        P.dma("sp", GA[:], dr["gains"], writes=aC)
        P.dma("sp", ONES[:], dr["ones_bf"], writes=aC)
        P.dma("sp", IDENT[:], dr["ident_bf"], writes=aC)
        P.dma("sp", FC[:], dr["fc_bf"], writes=aC)
        P.dma("sp", D0[:], dr["d0"], writes=aC)
        P.dma("sp", XIDX[:], dr["xidx"], writes=aC)
        P.dma("sp", TIDX[:], dr["tidx"], writes=aC)
        P.dma("sp", ZIDX[:], dr["zidx"], writes=aC)
        P.dma("sp", CIDX[:], dr["cidx"], writes=aC)
        P.dma("sp", DEC[:], dr["decay"].partition_broadcast(128), writes=aLG)
        P.op("act", lambda e: e.activation(out=DEC[:], in_=DEC[:], func=AF.Exp, scale=-1.0), reads=aLG, writes=aLG)
        P.op("act", lambda e: e.activation(out=NLG[:], in_=DEC[:], func=AF.Ln, bias=1.0), reads=aLG, writes=aLG)
        P.op("dve", lambda e: e.tensor_scalar(out=LG[:], in0=NLG[:], scalar1=-1.0, scalar2=None, op0=ALU.mult),
             reads=aLG, writes=aLG)

        def norm_stats(src_fn, n, atoms_src):
            P.op("act", lambda e: e.activation(out=SQ.ap[:, :, 0:n], in_=src_fn(), func=AF.Square),
                 reads=atoms_src, writes=SQ.all())
            b = nps()
            for k in range(KC):
                P.op("pe", lambda e, b=b, k=k: e.matmul(PS[b][:, 0:n], lhsT=ONES[:, :], rhs=SQ.ap[:, k, 0:n],
                                                      start=(k == 0), stop=(k == KC - 1)),
                     reads=SQ.all() + aC, writes=[aPS[b]], inc=(k == KC - 1))
            r = nxt("rs", 2)
            P.op("act", lambda e, b=b, r=r: e.activation(out=RS[:, r, 0:n], in_=PS[b][:, 0:n], func=AF.Ln,
                                                       scale=1.0 / D, bias=EPS),
                 reads=[aPS[b]], writes=[aRS[r]])
            P.op("act", lambda e, r=r: e.activation(out=RS[:, r, 0:n], in_=RS[:, r, 0:n], func=AF.Exp, scale=-0.5),
                 reads=[aRS[r]], writes=[aRS[r]])
            return r

        def rmsnorm_to_xn(gidx):
            for t in range(NT):
                xa = [aX[k][t] for k in range(KC)]
                r = norm_stats(lambda t=t: X[:, :, ts(t)], TT, xa)
                for k in range(KC):
                    col = gidx * KC + k
                    P.op("dve", lambda e, k=k, t=t, r=r, col=col: e.scalar_tensor_tensor(
                        out=XN[:, k, ts(t)], in0=X[:, k, ts(t)], scalar=GA[:, col:col + 1], in1=RS[:, r, :],
                        op0=ALU.mult, op1=ALU.mult),
                        reads=[aX[k][t], aRS[r]] + aC, writes=[aXN[t]])

        def load_w_block(wap, l, k0, nk, c0, swap=False):
            s = nxt("wr", NW)
            rows = wap[l, k0 * 128:(k0 + nk) * 128, :]
            if not swap:
                src = rows[:, c0:c0 + 128].rearrange("(k p) c -> p k c", p=128)
                P.dma("pool", WR[:, s, 0:nk, :], src, writes=[aWR[s]])
            else:
                P.dma("pool", WR[:, s, 0:nk, 0:64], rows[:, c0 + 64:c0 + 128].rearrange("(k p) c -> p k c", p=128),
                      writes=[aWR[s]])
                P.dma("pool", WR[:, s, 0:nk, 64:128], rows[:, c0:c0 + 64].rearrange("(k p) c -> p k c", p=128),
                      writes=[aWR[s]])
            return s

        def mm_group(b, s, nk, rhs_fn, rhs_atoms_fn, n=TT):
            for k in range(nk):
                P.op("pe", lambda e, b=b, s=s, k=k: e.matmul(
                    PS[b][:, 0:n], lhsT=WR[:, s, k, :], rhs=rhs_fn(k), start=(k == 0), stop=(k == nk - 1)),
                    reads=[aWR[s]] + rhs_atoms_fn(k), writes=[aPS[b]], inc=(k == nk - 1))

        def xn_rhs(t):
            return (lambda k: XN[:, k, ts(t)]), (lambda k: [aXN[t]])

        def x_add(b, mt, t, scale):
            P.op("dve", lambda e: e.scalar_tensor_tensor(
                out=X[:, mt, ts(t)], in0=PS[b][:, :], scalar=scale, in1=X[:, mt, ts(t)],
                op0=ALU.mult, op1=ALU.add),
                reads=[aPS[b], aX[mt][t]], writes=[aX[mt][t]])

        def ffn(l, w_in, w_out, gidx):
            rmsnorm_to_xn(l * 5 + gidx)
            for hh in range(2):
                for jj in range(NH):
                    j = hh * NH + jj
                    sg = load_w_block(w_in, l, 0, KC, j * 128)
                    su = load_w_block(w_in, l, 0, KC, DFF + j * 128)
                    for t in range(NT):
                        rf, ra = xn_rhs(t)
                        bg = nps()
                        mm_group(bg, sg, KC, rf, ra)
                        bu = nps()
                        mm_group(bu, su, KC, rf, ra)
                        m = nxt("tmp", NTMP)
                        P.op("act", lambda e, bg=bg, m=m: e.activation(out=TMP[:, m, :], in_=PS[bg][:, :], func=AF.Silu),
                             reads=[aPS[bg]], writes=[aTMP[m]])
                        P.op("dve", lambda e, bu=bu, m=m, jj=jj, t=t: e.tensor_tensor(
                            out=HID.ap[:, jj, ts(t)], in0=PS[bu][:, :], in1=TMP[:, m, :], op=ALU.mult),
                            reads=[aPS[bu], aTMP[m]], writes=HID.at(jj, e0=t * TT, e1=(t + 1) * TT))
                for mt in range(KC):
                    s1 = load_w_block(w_out, l, hh * NH, KC, mt * 128)
                    s2 = load_w_block(w_out, l, hh * NH + KC, NH - KC, mt * 128)
                    for t in range(NT):
                        b = nps()
                        for jj in range(NH):
                            s, kk = (s1, jj) if jj < KC else (s2, jj - KC)
                            P.op("pe", lambda e, b=b, s=s, kk=kk, jj=jj, t=t: e.matmul(
                                PS[b][:, :], lhsT=WR[:, s, kk, :], rhs=HID.ap[:, jj, ts(t)],
                                start=(jj == 0), stop=(jj == NH - 1)),
                                reads=[aWR[s]] + HID.at(jj, e0=t * TT, e1=(t + 1) * TT), writes=[aPS[b]],
                                inc=(jj == NH - 1))
                        x_add(b, mt, t, 0.5)

        def xa_block(l, g):
            P.dma("pool", MEMN.ap[:, :, :], dr["memT_" + g].rearrange("(k p) m -> p k m", p=128), writes=MEMN.all())
            r = norm_stats(lambda: MEMN.ap[:, :, :], NMEM, MEMN.all())
            for k in range(KC):
                col = (l * 5 + G_MEM) * KC + k
                P.op("dve", lambda e, k=k, r=r, col=col: e.scalar_tensor_tensor(
                    out=MEMN.ap[:, k, :], in0=MEMN.ap[:, k, :], scalar=GA[:, col:col + 1], in1=RS[:, r, 0:NMEM],
                    op0=ALU.mult, op1=ALU.mult),
                    reads=MEMN.all() + [aRS[r]] + aC, writes=MEMN.all())
            for m in range(KC):
                s = load_w_block(dr["xa_wkv"], l, 0, KC, m * 128)
                b = nps()
                mm_group(b, s, KC, lambda k: MEMN.ap[:, k, :], lambda k: MEMN.all(), n=NMEM)
                P.op("act", lambda e, b=b, m=m: e.activation(out=KTX.ap[:, m, :], in_=PS[b][:, 0:NMEM], func=AF.Copy),
                     reads=[aPS[b]], writes=KTX.all())
            for n2 in range(2):
                P.dma("pool", WS.ap[:, :, :],
                      dr["xa_wkv"][l, :, D + n2 * TT:D + (n2 + 1) * TT].rearrange("(k p) c -> p k c", p=128),
                      writes=WS.all())
                for mt in range(2):
                    b = nps()
                    for k in range(KC):
                        P.op("pe", lambda e, b=b, k=k, mt=mt: e.matmul(
                            PS[b][:, :], lhsT=MEMN.ap[:, k, mt * 128:(mt + 1) * 128], rhs=WS.ap[:, k, :],
                            start=(k == 0), stop=(k == KC - 1)),
                            reads=MEMN.all() + WS.all(), writes=[aPS[b]], inc=(k == KC - 1))
                    P.op("act", lambda e, b=b, mt=mt, n2=n2: e.activation(
                        out=VX.ap[:, mt, n2 * TT:(n2 + 1) * TT], in_=PS[b][:, :], func=AF.Copy),
                        reads=[aPS[b]], writes=VX.all())
            rmsnorm_to_xn(l * 5 + G_XA)
            for m in range(KC):
                s = load_w_block(dr["xa_wq"], l, 0, KC, m * 128)
                for t in range(NT):
                    rf, ra = xn_rhs(t)
                    b = nps()
                    mm_group(b, s, KC, rf, ra)
                    P.op("act", lambda e, b=b, m=m, t=t: e.activation(
                        out=QX.ap[:, m, ts(t)], in_=PS[b][:, :], func=AF.Copy, scale=1.0 / 16.0),
                        reads=[aPS[b]], writes=QX.at(m, e0=t * TT, e1=(t + 1) * TT))
            stt = {}

            def st_scores(u):
                tok = slice(u * 128, (u + 1) * 128)
                qa = QX.at(0, KC, u * 128, (u + 1) * 128)
                bs = [nps(), nps()]
                for h in range(4):
                    for dc in range(2):
                        P.op("pe", lambda e, h=h, dc=dc, bs=bs, tok=tok: e.matmul(
                            PS[bs[h // 2]][:, (h % 2) * NMEM:(h % 2 + 1) * NMEM], lhsT=QX.ap[:, h * 2 + dc, tok],
                            rhs=KTX.ap[:, h * 2 + dc, :], start=(dc == 0), stop=(dc == 1)),
                            reads=qa + KTX.all(), writes=[aPS[bs[h // 2]]])
                stt[u] = {"bs": bs}

            def st_softmax(u):
                bs = stt[u]["bs"]
                z = nxt("sm2", 2)
                for hp in range(2):
                    P.op("dve", lambda e, hp=hp, z=z, bs=bs: e.reduce_max(
                        out=SM2[:, z, hp * 2:hp * 2 + 2], in_=PS[bs[hp]][:, :].rearrange("p (a b) -> p a b", a=2),
                        axis=AX.X), reads=[aPS[bs[hp]]], writes=[aSM2[z]])
                P.op("dve", lambda e, z=z: e.tensor_scalar(out=SM2[:, z, 0:4], in0=SM2[:, z, 0:4], scalar1=-1.0,
                                                          scalar2=None, op0=ALU.mult),
                     reads=[aSM2[z]], writes=[aSM2[z]])
                pb = nxt("pb", 2)
                for h in range(4):
                    P.op("act", lambda e, h=h, z=z, pb=pb, bs=bs: e.activation(
                        out=PB[pb].ap[:, h, :], in_=PS[bs[h // 2]][:, (h % 2) * NMEM:(h % 2 + 1) * NMEM],
                        func=AF.Exp, bias=SM2[:, z, h:h + 1], scale=1.0, accum_out=SM2[:, z, 4 + h:5 + h]),
                        reads=[aPS[bs[h // 2]], aSM2[z]], writes=PB[pb].all() + [aSM2[z]])
                P.op("dve", lambda e, z=z: e.reciprocal(out=SM2[:, z, 8:12], in_=SM2[:, z, 4:8]),
                     reads=[aSM2[z]], writes=[aSM2[z]])
                for h in range(4):
                    P.op("dve", lambda e, h=h, z=z, pb=pb: e.tensor_scalar(
                        out=PB[pb].ap[:, h, :], in0=PB[pb].ap[:, h, :], scalar1=SM2[:, z, 8 + h:9 + h], scalar2=None,
                        op0=ALU.mult), reads=PB[pb].all() + [aSM2[z]], writes=PB[pb].all())
                stt[u]["pb"] = pb

            def st_transpose(u):
                pb = stt[u]["pb"]
                bt = nps()
                ptv = PS[bt][:, :].bitcast(BF16).rearrange("p (a b) -> p a b", b=128)
                for h in range(4):
                    for mt in range(2):
                        P.op("pe", lambda e, h=h, mt=mt, pb=pb, ptv=ptv: e.transpose(
                            out=ptv[:, h * 2 + mt, :], in_=PB[pb].ap[:, h, mt * 128:(mt + 1) * 128], identity=IDENT[:, :]),
                            reads=PB[pb].all() + aC, writes=[aPS[bt]])
                px = nxt("ptx", 2)
                P.op("act", lambda e, px=px, ptv=ptv: e.activation(out=PTX[px].ap[:, :, :], in_=ptv[:, 0:8, :], func=AF.Copy),
                     reads=[aPS[bt]], writes=PTX[px].all())
                stt[u]["px"] = px

            def st_pv(u):
                tok = slice(u * 128, (u + 1) * 128)
                px = stt[u]["px"]
                bo = [nps(), nps()]
                for h in range(4):
                    for dc in range(2):
                        idx = h * 2 + dc
                        for mt in range(2):
                            P.op("pe", lambda e, h=h, dc=dc, mt=mt, idx=idx, px=px, bo=bo: e.matmul(
                                PS[bo[idx // 4]][:, (idx % 4) * 128:(idx % 4 + 1) * 128],
                                lhsT=VX.ap[:, mt, h * 256 + dc * 128:h * 256 + (dc + 1) * 128],
                                rhs=PTX[px].ap[:, h * 2 + mt, :], start=(mt == 0), stop=(mt == 1)),
                                reads=VX.all() + PTX[px].all(), writes=[aPS[bo[idx // 4]]])
                for hf in range(2):
                    P.op("act", lambda e, hf=hf, bo=bo, tok=tok: e.activation(
                        out=QX.ap[:, hf * 4:(hf + 1) * 4, tok], in_=PS[bo[hf]][:, :].rearrange("p (a b) -> p a b", a=4),
                        func=AF.Copy), reads=[aPS[bo[hf]]], writes=QX.at(hf * 4, hf * 4 + 4, u * 128, (u + 1) * 128))
                del stt[u]

            for step in range(NCH + 2):
                if step < NCH:
                    st_scores(step)
                    st_softmax(step)
                if 0 <= step - 1 < NCH:
                    st_transpose(step - 1)
                if 0 <= step - 2 < NCH:
                    st_pv(step - 2)
            for m in range(KC):
                s = load_w_block(dr["xa_wo"], l, 0, KC, m * 128)
                for t in range(NT):
                    b = nps()
                    mm_group(b, s, KC, lambda k, t=t: QX.ap[:, k, ts(t)],
                             lambda k, t=t: QX.at(k, e0=t * TT, e1=(t + 1) * TT))
                    x_add(b, m, t, 1.0)

        def fourier(l, g):
            for gg in range(4):
                s = load_w_block(dr["mix_w_in"], l, 0, KC, OFF_HF + gg * 128)
                for t in range(NT):
                    rf, ra = xn_rhs(t)
                    b = nps()
                    mm_group(b, s, KC, rf, ra)
                    P.op("act", lambda e, b=b, gg=gg, t=t: e.activation(out=HFY.ap[:, gg, ts(t)], in_=PS[b][:, :], func=AF.Copy),
                         reads=[aPS[b]], writes=HFY.at(gg, e0=t * TT, e1=(t + 1) * TT))
            for j in range(NCH):
                bb = [nps(), nps()]
                for gg in range(4):
                    P.op("pe", lambda e, gg=gg, j=j, bb=bb: e.matmul(
                        PS[bb[gg // 2]][:, (gg % 2) * 256:(gg % 2 + 1) * 256], lhsT=HFY.ap[:, gg, j * 128:(j + 1) * 128],
                        rhs=FC[:, :], start=True, stop=True),
                        reads=HFY.at(gg, e0=j * 128, e1=(j + 1) * 128) + aC, writes=[aPS[bb[gg // 2]]])
                q = nxt("pst", 2)
                for hf in range(2):
                    P.op("act" if hf == 0 else "dve",
                         (lambda e, hf=hf, q=q, bb=bb: e.activation(out=PST[q].ap[:, hf * TT:(hf + 1) * TT], in_=PS[bb[hf]][:, :], func=AF.Copy))
                         if hf == 0 else
                         (lambda e, hf=hf, q=q, bb=bb: e.tensor_copy(out=PST[q].ap[:, hf * TT:(hf + 1) * TT], in_=PS[bb[hf]][:, :])),
                         reads=[aPS[bb[hf]]], writes=PST[q].all())
                P.dma("sp", PSCR.ap()[j * 128:(j + 1) * 128, :], PST[q].ap[:, :], reads=PST[q].all(), writes=[aPSCR[j]])
            if g == "s":
                P.allgather(PGATH.ap(), PSCR.ap(), reads=aPSCR, writes=aPG)
                src, src_atoms, npiece, E = PGATH.ap(), aPG, SSAMP // 512, dr["E_s"]
            else:
                src, src_atoms, npiece, E = PSCR.ap(), aPSCR, T // 512, dr["E_p"]
            srcv = src.rearrange("(j p) c -> p j c", p=128)
            for kr in range(4):
                Ev = E[kr].rearrange("(j p) s k -> p j s k", p=128)
                acc = [nps(hold=True) for _ in range(4)]
                for pc in range(npiece):
                    sl = nxt("fpc", 3)
                    P.dma("sp", PPC[sl].ap[:, :, :], srcv[:, pc * 4:(pc + 1) * 4, :], reads=src_atoms, writes=PPC[sl].all())
                    P.dma("sp", EPC[sl].ap[:, :, :].rearrange("p (j s) k -> p j s k", s=2), Ev[:, pc * 4:(pc + 1) * 4, :, :],
                          writes=EPC[sl].all())
                    for jj in range(4):
                        for gg in range(4):
                            for ri in range(2):
                                first = (pc == 0 and jj == 0 and ri == 0)
                                last = (pc == npiece - 1 and jj == 3 and ri == 1)
                                P.op("pe", lambda e, sl=sl, jj=jj, gg=gg, ri=ri, first=first, last=last, acc=acc: e.matmul(
                                    PS[acc[gg]][:, :], lhsT=PPC[sl].ap[:, jj, gg * 256 + ri * 128:gg * 256 + (ri + 1) * 128],
                                    rhs=EPC[sl].ap[:, jj * 2 + ri, :], start=first, stop=last),
                                    reads=PPC[sl].all() + EPC[sl].all(), writes=[aPS[acc[gg]]])
                for gg in range(4):
                    P.op("act" if gg % 2 == 0 else "dve",
                         (lambda e, gg=gg, kr=kr, acc=acc: e.activation(out=HFY.ap[:, gg, ts(kr)], in_=PS[acc[gg]][:, :], func=AF.Copy))
                         if gg % 2 == 0 else
                         (lambda e, gg=gg, kr=kr, acc=acc: e.tensor_copy(out=HFY.ap[:, gg, ts(kr)], in_=PS[acc[gg]][:, :])),
                         reads=[aPS[acc[gg]]], writes=HFY.at(gg, e0=kr * TT, e1=(kr + 1) * TT))
                for b_ in acc:
                    held.discard(b_)

        def proj_rot(l, g, h, off, dst):
            s1 = load_w_block(dr["mix_w_in"], l, 0, KC, off + h * 128)
            s2 = load_w_block(dr["mix_w_in"], l, 0, KC, off + h * 128, swap=True)
            rot = dr["rot_" + g]
            for t in range(NT):
                rb = nxt("rot", 1)
                P.dma("sp", ROTB[:, rb, :, :], rot[:, :, ts(t)].rearrange("s p t -> p s t"), writes=[aROT[rb]])
                rf, ra = xn_rhs(t)
                b1 = nps()
                mm_group(b1, s1, KC, rf, ra)
                b2 = nps()
                mm_group(b2, s2, KC, rf, ra)
                m1 = nxt("tmp", NTMP)
                P.op("dve", lambda e, b1=b1, m1=m1, rb=rb: e.tensor_tensor(
                    out=TMP[:, m1, :], in0=PS[b1][:, :], in1=ROTB[:, rb, 0, :], op=ALU.mult),
                    reads=[aPS[b1], aROT[rb]], writes=[aTMP[m1]])
                m2 = nxt("tmp", NTMP)
                P.op("dve", lambda e, b2=b2, m2=m2, rb=rb: e.tensor_tensor(
                    out=TMP[:, m2, :], in0=PS[b2][:, :], in1=ROTB[:, rb, 1, :], op=ALU.mult),
                    reads=[aPS[b2], aROT[rb]], writes=[aTMP[m2]])
                P.op("dve", lambda e, m1=m1, m2=m2, t=t: e.tensor_tensor(
                    out=dst.ap[:, ts(t)], in0=TMP[:, m1, :], in1=TMP[:, m2, :], op=ALU.add),
                    reads=[aTMP[m1], aTMP[m2]], writes=dst.at(e0=t * TT, e1=(t + 1) * TT))

        def retention_head(l, g, h):
            cf, cb = l * 4 + h, 16 + l * 4 + h
            lgf, lgb = LG[:, cf:cf + 1], LG[:, cb:cb + 1]
            nlgb = NLG[:, cb:cb + 1]
            proj_rot(l, g, h, OFF_K, KTH)
            P.dma("pool", WV.ap[:, :, :],
                  dr["mix_w_in"][l, :, OFF_V + h * 256:OFF_V + (h + 1) * 256].rearrange("(k p) c -> p k c", p=128),
                  writes=WV.all())
            for j in range(NCH):
                b = nps()
                for k in range(KC):
                    P.op("pe", lambda e, b=b, k=k, j=j: e.matmul(
                        PS[b][:, 0:256], lhsT=XN[:, k, j * 128:(j + 1) * 128], rhs=WV.ap[:, k, :],
                        start=(k == 0), stop=(k == KC - 1)),
                        reads=[aXN[j // 4]] + WV.all(), writes=[aPS[b]], inc=(k == KC - 1))
                P.op("act", lambda e, b=b, j=j: e.activation(out=VH.ap[:, j, :], in_=PS[b][:, 0:256], func=AF.Copy),
                     reads=[aPS[b]], writes=VH.at(j))
            P.op("act", lambda e: e.activation(out=SM[:, 0:16], in_=TIDX[:, :], func=AF.Exp, scale=lgf),
                 reads=aC + aLG, writes=aSM)
            P.op("act", lambda e: e.activation(out=SM[:, 16:32], in_=TIDX[:, :], func=AF.Exp, scale=lgb),
                 reads=aC + aLG, writes=aSM)
            P.op("dve", lambda e: e.tensor_scalar(out=SM[:, 32:33], in0=lgf, scalar1=128.0, scalar2=LN_S,
                                                  op0=ALU.mult, op1=ALU.add), reads=aLG, writes=aSM)
            P.op("dve", lambda e: e.tensor_scalar(out=SM[:, 33:34], in0=lgb, scalar1=512.0, scalar2=LN_S,
                                                  op0=ALU.mult, op1=ALU.add), reads=aLG, writes=aSM)
            P.op("act", lambda e: e.activation(out=TAB.ap[:, 0, :], in_=D0[:, :], func=AF.Exp, scale=lgf, bias=SM[:, 32:33]),
                 reads=aC + aLG + aSM, writes=TAB.at(0))
            P.op("act", lambda e: e.activation(out=TAB.ap[:, 1, :], in_=D0[:, :], func=AF.Exp, scale=nlgb, bias=SM[:, 33:34]),
                 reads=aC + aLG + aSM, writes=TAB.at(1))
            for r in range(4):
                m1, m2 = nxt("tmp", NTMP), nxt("tmp", NTMP)
                P.op("dve", lambda e, r=r, m1=m1: e.tensor_scalar(
                    out=TMP[:, m1, :], in0=D0[:, :], scalar1=-128.0 * r, scalar2=0.0, op0=ALU.add, op1=ALU.max),
                    reads=aC, writes=[aTMP[m1]])
                P.op("dve", lambda e, m1=m1: e.tensor_scalar(
                    out=TMP[:, m1, :], in0=TMP[:, m1, :], scalar1=lgf, scalar2=None, op0=ALU.mult),
                    reads=[aTMP[m1]] + aLG, writes=[aTMP[m1]])
                P.op("dve", lambda e, r=r, m2=m2: e.tensor_scalar(
                    out=TMP[:, m2, :], in0=D0[:, :], scalar1=-128.0 * r, scalar2=0.0, op0=ALU.add, op1=ALU.min),
                    reads=aC, writes=[aTMP[m2]])
                P.op("dve", lambda e, m1=m1, m2=m2: e.scalar_tensor_tensor(
                    out=TMP[:, m1, :], in0=TMP[:, m2, :], scalar=nlgb, in1=TMP[:, m1, :], op0=ALU.mult, op1=ALU.add),
                    reads=[aTMP[m1], aTMP[m2]] + aLG, writes=[aTMP[m1]])
                P.op("act", lambda e, r=r, m1=m1: e.activation(out=TAB.ap[:, 2 + r, :], in_=TMP[:, m1, :], func=AF.Exp,
                                                             bias=LN_S), reads=[aTMP[m1]], writes=TAB.at(2 + r))
            if g == "s":
                P.op("act", lambda e: e.activation(out=SM[:, 64:80], in_=ZIDX[:, 0, :], func=AF.Exp, scale=lgf),
                     reads=aC + aLG, writes=aSM)
                P.op("act", lambda e: e.activation(out=SM[:, 80:96], in_=ZIDX[:, 1, :], func=AF.Exp, scale=lgb),
                     reads=aC + aLG, writes=aSM)
                P.op("act", lambda e: e.activation(out=SM[:, 48:56], in_=CIDX[:, 0, :], func=AF.Exp, scale=lgf),
                     reads=aC + aLG, writes=aSM)
                P.op("act", lambda e: e.activation(out=SM[:, 56:64], in_=CIDX[:, 2, :], func=AF.Exp, scale=lgb),
                     reads=aC + aLG, writes=aSM)
                P.op("dve", lambda e: e.tensor_tensor(out=SM[:, 48:56], in0=SM[:, 48:56], in1=CIDX[:, 1, :], op=ALU.mult),
                     reads=aSM + aC, writes=aSM)
                P.op("dve", lambda e: e.tensor_tensor(out=SM[:, 56:64], in0=SM[:, 56:64], in1=CIDX[:, 3, :], op=ALU.mult),
                     reads=aSM + aC, writes=aSM)
                P.op("dve", lambda e: e.tensor_scalar(out=SM[:, 34:38], in0=TIDX[:, 0:16:4], scalar1=lgf, scalar2=None,
                                                      op0=ALU.mult), reads=aC + aLG, writes=aSM)
                P.op("dve", lambda e: e.tensor_scalar(out=SM[:, 38:42], in0=TIDX[:, 0:16:4], scalar1=-1.0, scalar2=2049.0,
                                                      op0=ALU.mult, op1=ALU.add), reads=aC, writes=aSM)
                P.op("dve", lambda e: e.tensor_scalar(out=SM[:, 38:42], in0=SM[:, 38:42], scalar1=lgb, scalar2=None,
                                                      op0=ALU.mult), reads=aSM + aLG, writes=aSM)
                bu = nps(hold=True)
                for j in range(NCH):
                    bt = nps()
                    ktv = PS[bt][:, :].bitcast(BF16)
                    P.op("pe", lambda e, j=j, ktv=ktv: e.transpose(out=ktv[:, 0:128], in_=KTH.ap[:, j * 128:(j + 1) * 128],
                                                                 identity=IDENT[:, :]),
                         reads=KTH.at(e0=j * 128, e1=(j + 1) * 128) + aC, writes=[aPS[bt]])
                    kb = nxt("kz", 2)
                    P.op("act", lambda e, j=j, kb=kb, ktv=ktv: e.activation(
                        out=KZ.ap[:, kb * 2, :], in_=ktv[:, 0:128], func=AF.Copy, scale=SM[:, 64 + j:65 + j]),
                        reads=[aPS[bt]] + aSM, writes=KZ.all())
                    P.op("dve", lambda e, j=j, kb=kb, ktv=ktv: e.tensor_scalar(
                        out=KZ.ap[:, kb * 2 + 1, :], in0=ktv[:, 0:128], scalar1=SM[:, 80 + j:81 + j], scalar2=None,
                        op0=ALU.mult), reads=[aPS[bt]] + aSM, writes=KZ.all())
                    for dd in range(2):
                        P.op("pe", lambda e, j=j, kb=kb, dd=dd, bu=bu: e.matmul(
                            PS[bu][:, dd * 256:(dd + 1) * 256], lhsT=KZ.ap[:, kb * 2 + dd, :], rhs=VH.ap[:, j, :],
                            start=(j == 0), stop=(j == NCH - 1)),
                            reads=KZ.all() + VH.at(j), writes=[aPS[bu]])
                held.discard(bu)
                mu = nxt("tmp", NTMP)
                P.op("act", lambda e, mu=mu, bu=bu: e.activation(out=TMP[:, mu, :], in_=PS[bu][:, :], func=AF.Copy),
                     reads=[aPS[bu]], writes=[aTMP[mu]])
                P.dma("sp", UBN.ap(), TMP[:, mu, :], reads=[aTMP[mu]], writes=aUB)
                P.allgather(UGATH.ap(), UBN.ap(), reads=aUB, writes=aUG)
            proj_rot(l, g, h, OFF_Q, QTH)
            if g == "s":
                mr = nxt("tmp", NTMP)
                for r in range(NCORES):
                    ml = nxt("tmp", NTMP)
                    if ml == mr:
                        ml = nxt("tmp", NTMP)
                    P.dma("sp", TMP[:, ml, :], UGATH.ap()[r * 128:(r + 1) * 128, :], reads=aUG, writes=[aTMP[ml]])
                    for dd in range(2):
                        cs = slice(dd * 256, (dd + 1) * 256)
                        col = 48 + dd * 8 + r
                        if r == 0:
                            P.op("dve", lambda e, ml=ml, mr=mr, cs=cs, col=col: e.tensor_scalar(
                                out=TMP[:, mr, cs], in0=TMP[:, ml, cs], scalar1=SM[:, col:col + 1], scalar2=None,
                                op0=ALU.mult), reads=[aTMP[ml]] + aSM, writes=[aTMP[mr]])
                        else:
                            P.op("dve", lambda e, ml=ml, mr=mr, cs=cs, col=col: e.scalar_tensor_tensor(
                                out=TMP[:, mr, cs], in0=TMP[:, ml, cs], scalar=SM[:, col:col + 1], in1=TMP[:, mr, cs],
                                op0=ALU.mult, op1=ALU.add), reads=[aTMP[ml], aTMP[mr]] + aSM, writes=[aTMP[mr]])
                P.op("act", lambda e, mr=mr: e.activation(out=RINB.ap[:, :], in_=TMP[:, mr, :], func=AF.Copy),
                     reads=[aTMP[mr]], writes=RINB.all())
            for i in range(NT):
                if g == "s":
                    for dd, dst in ((0, QXF), (1, QXB)):
                        mx = nxt("tmp", NTMP)
                        sc = lgf if dd == 0 else nlgb
                        bcol = 34 + dd * 4 + i
                        P.op("act", lambda e, mx=mx, sc=sc, bcol=bcol: e.activation(
                            out=TMP[:, mx, :], in_=XIDX[:, :], func=AF.Exp, scale=sc, bias=SM[:, bcol:bcol + 1]),
                            reads=aC + aLG + aSM, writes=[aTMP[mx]])
                        P.op("dve", lambda e, mx=mx, dst=dst, i=i: e.tensor_tensor(
                            out=dst.ap[:, :], in0=QTH.ap[:, ts(i)], in1=TMP[:, mx, :], op=ALU.mult),
                            reads=QTH.at(e0=i * TT, e1=(i + 1) * TT) + [aTMP[mx]], writes=dst.all())
                by = [nps(hold=True), nps(hold=True)]
                sbank = {}

                def emit_s(j, i=i):
                    bs = nps()
                    sbank[j] = bs
                    P.op("pe", lambda e, bs=bs, j=j, i=i: e.matmul(
                        PS[bs][:, :], lhsT=KTH.ap[:, j * 128:(j + 1) * 128], rhs=QTH.ap[:, ts(i)], start=True, stop=True),
                        reads=KTH.at(e0=j * 128, e1=(j + 1) * 128) + QTH.at(e0=i * TT, e1=(i + 1) * TT), writes=[aPS[bs]])

                LOOK = 2
                for j in range(min(LOOK, NCH)):
                    emit_s(j)
                for j in range(NCH):
                    bs = sbank.pop(j)
                    ab = nxt("atb", 3)
                    if j <= 4 * i - 1:
                        tcol, tb = 4 * i - j - 1, 0
                    elif j >= 4 * i + 4:
                        tcol, tb = 16 + j - 4 * i - 4, 1
                    else:
                        tcol, tb = None, 2 + (j - 4 * i)
                    if tcol is None:
                        P.op("dve", lambda e, bs=bs, ab=ab, tb=tb: e.tensor_tensor(
                            out=ATB[ab].ap[:, :], in0=PS[bs][:, :], in1=TAB.ap[:, tb, :], op=ALU.mult),
                            reads=[aPS[bs]] + TAB.at(tb), writes=ATB[ab].all())
                    else:
                        P.op("dve", lambda e, bs=bs, ab=ab, tb=tb, tcol=tcol: e.scalar_tensor_tensor(
                            out=ATB[ab].ap[:, :], in0=PS[bs][:, :], scalar=SM[:, tcol:tcol + 1], in1=TAB.ap[:, tb, :],
                            op0=ALU.mult, op1=ALU.mult),
                            reads=[aPS[bs]] + TAB.at(tb) + aSM, writes=ATB[ab].all())
                    if j + LOOK < NCH:
                        emit_s(j + LOOK)
                    for ee in range(2):
                        last = (j == NCH - 1) and g != "s"
                        P.op("pe", lambda e, ee=ee, j=j, ab=ab, by=by, last=last: e.matmul(
                            PS[by[ee]][:, :], lhsT=VH.ap[:, j, ee * 128:(ee + 1) * 128], rhs=ATB[ab].ap[:, :],
                            start=(j == 0), stop=last),
                            reads=VH.at(j) + ATB[ab].all(), writes=[aPS[by[ee]]])
                if g == "s":
                    for ee in range(2):
                        for dd, srcq in ((0, QXF), (1, QXB)):
                            P.op("pe", lambda e, ee=ee, dd=dd, srcq=srcq, by=by: e.matmul(
                                PS[by[ee]][:, :], lhsT=RINB.ap[:, dd * 256 + ee * 128:dd * 256 + (ee + 1) * 128],
                                rhs=srcq.ap[:, :], start=False, stop=(dd == 1)),
                                reads=RINB.all() + srcq.all(), writes=[aPS[by[ee]]])
                held.discard(by[0])
                held.discard(by[1])
                for ee in range(2):
                    P.op("act", lambda e, ee=ee, by=by: e.activation(out=SQ2.ap[:, ee, :], in_=PS[by[ee]][:, :], func=AF.Square),
                         reads=[aPS[by[ee]]], writes=SQ2.all())
                bn = nps()
                for ee in range(2):
                    P.op("pe", lambda e, ee=ee, bn=bn: e.matmul(PS[bn][:, :], lhsT=ONES[:, :], rhs=SQ2.ap[:, ee, :],
                                                              start=(ee == 0), stop=(ee == 1)),
                         reads=SQ2.all() + aC, writes=[aPS[bn]])
                r = nxt("rs", 2)
                P.op("act", lambda e, bn=bn, r=r: e.activation(out=RS[:, r, :], in_=PS[bn][:, :], func=AF.Ln,
                                                             scale=1.0 / 256.0, bias=EPS),
                     reads=[aPS[bn]], writes=[aRS[r]])
                P.op("act", lambda e, r=r: e.activation(out=RS[:, r, :], in_=RS[:, r, :], func=AF.Exp, scale=-0.5),
                     reads=[aRS[r]], writes=[aRS[r]])
                for ee in range(2):
                    mo = nxt("tmp", NTMP)
                    row = h * 2 + ee
                    P.op("dve", lambda e, ee=ee, by=by, mo=mo, r=r: e.tensor_tensor(
                        out=TMP[:, mo, :], in0=PS[by[ee]][:, :], in1=RS[:, r, :], op=ALU.mult),
                        reads=[aPS[by[ee]], aRS[r]], writes=[aTMP[mo]])
                    P.op("dve", lambda e, mo=mo, row=row, i=i: e.tensor_tensor(
                        out=RT.ap[:, row, ts(i)], in0=TMP[:, mo, :], in1=RT.ap[:, row, ts(i)], op=ALU.mult),
                        reads=[aTMP[mo]] + RT.at(row, e0=i * TT, e1=(i + 1) * TT),
                        writes=RT.at(row, e0=i * TT, e1=(i + 1) * TT))

        def mix_block(l, g):
            rmsnorm_to_xn(l * 5 + G_MIX)
            if do_fourier:
                fourier(l, g)
            for m in range(KC):
                s = load_w_block(dr["mix_w_in"], l, 0, KC, OFF_G + m * 128)
                for t in range(NT):
                    rf, ra = xn_rhs(t)
                    b = nps()
                    mm_group(b, s, KC, rf, ra)
                    P.op("act", lambda e, b=b, m=m, t=t: e.activation(out=RT.ap[:, m, ts(t)], in_=PS[b][:, :], func=AF.Silu),
                         reads=[aPS[b]], writes=RT.at(m, e0=t * TT, e1=(t + 1) * TT))
            if do_ret:
                for h in range(4):
                    retention_head(l, g, h)
            for m in range(KC):
                sa = load_w_block(dr["mix_w_in"], l, 0, KC, OFF_GA + m * 128)
                sbb = load_w_block(dr["mix_w_in"], l, 0, KC, OFF_GB + m * 128)
                sf = load_w_block(dr["fourier_w"], l, 0, 4, m * 128)
                sr = load_w_block(dr["ret_w_out"], l, 0, KC, m * 128)
                for t in range(NT):
                    rf, ra = xn_rhs(t)
                    ba = nps()
                    mm_group(ba, sa, KC, rf, ra)
                    bb = nps()
                    mm_group(bb, sbb, KC, rf, ra)
                    bf = nps()
                    mm_group(bf, sf, 4, lambda k, t=t: HFY.ap[:, k, ts(t)], lambda k, t=t: HFY.at(k, e0=t * TT, e1=(t + 1) * TT))
                    br = nps()
                    mm_group(br, sr, KC, lambda k, t=t: RT.ap[:, k, ts(t)], lambda k, t=t: RT.at(k, e0=t * TT, e1=(t + 1) * TT))
                    m1, m2 = nxt("tmp", NTMP), nxt("tmp", NTMP)
                    P.op("act", lambda e, ba=ba, m1=m1: e.activation(out=TMP[:, m1, :], in_=PS[ba][:, :], func=AF.Sigmoid),
                         reads=[aPS[ba]], writes=[aTMP[m1]])
                    P.op("act", lambda e, bb=bb, m2=m2: e.activation(out=TMP[:, m2, :], in_=PS[bb][:, :], func=AF.Sigmoid),
                         reads=[aPS[bb]], writes=[aTMP[m2]])
                    P.op("dve", lambda e, bf=bf, m1=m1: e.tensor_tensor(out=TMP[:, m1, :], in0=PS[bf][:, :], in1=TMP[:, m1, :],
                                                                       op=ALU.mult), reads=[aPS[bf], aTMP[m1]], writes=[aTMP[m1]])
                    P.op("dve", lambda e, br=br, m2=m2: e.tensor_tensor(out=TMP[:, m2, :], in0=PS[br][:, :], in1=TMP[:, m2, :],
                                                                       op=ALU.mult), reads=[aPS[br], aTMP[m2]], writes=[aTMP[m2]])
                    P.op("dve", lambda e, m1=m1, m2=m2, m=m, t=t: e.tensor_tensor(
                        out=MERGED.ap[:, m, ts(t)], in0=TMP[:, m1, :], in1=TMP[:, m2, :], op=ALU.add),
                        reads=[aTMP[m1], aTMP[m2]], writes=MERGED.at(m, e0=t * TT, e1=(t + 1) * TT))
            for m in range(KC):
                s = load_w_block(dr["mix_w_out"], l, 0, KC, m * 128)
                for t in range(NT):
                    b = nps()
                    mm_group(b, s, KC, lambda k, t=t: MERGED.ap[:, k, ts(t)],
                             lambda k, t=t: MERGED.at(k, e0=t * TT, e1=(t + 1) * TT))
                    x_add(b, m, t, 1.0)

        for g in groups:
            xin = dr["xT_" + g].rearrange("(k p) t -> p k t", p=128)
            for t in range(NT):
                P.dma("sp", X[:, :, ts(t)], xin[:, :, ts(t)], writes=[aX[k][t] for k in range(KC)])
            for l in range(n_layers):
                if do_ffn:
                    ffn(l, dr["ffn1_w_in"], dr["ffn1_w_out"], G_FFN1)
                if do_mix:
                    mix_block(l, g)
                if do_xa:
                    xa_block(l, g)
                if do_ffn:
                    ffn(l, dr["ffn2_w_in"], dr["ffn2_w_out"], G_FFN2)
            yout = yT[g].rearrange("(k p) t -> p k t", p=128)
            for t in range(NT):
                xa_ = [aX[k][t] for k in range(KC)]
                r = norm_stats(lambda t=t: X[:, :, ts(t)], TT, xa_)
                for k in range(KC):
                    col = 20 * KC + k
                    o = nxt("tmp", NTMP)
                    P.op("dve", lambda e, k=k, t=t, r=r, col=col, o=o: e.scalar_tensor_tensor(
                        out=TMP[:, o, :], in0=X[:, k, ts(t)], scalar=GA[:, col:col + 1], in1=RS[:, r, :],
                        op0=ALU.mult, op1=ALU.mult),
                        reads=[aX[k][t], aRS[r]] + aC, writes=[aTMP[o]])
                    P.dma("sp", yout[:, k, ts(t)], TMP[:, o, :], reads=[aTMP[o]], writes=aOUT)
        P.finish(aOUT)

        @block.tensor
        def _(e):
            P.replay("pe", e)

        @block.scalar
        def _(e):
            P.replay("act", e)

        @block.vector
        def _(e):
            P.replay("dve", e)

        @block.gpsimd
        def _(e):
            P.replay("pool", e)

        @block.sync
        def _(e):
            P.replay("sp", e)

    return nc


_CONST_CACHE = {}


def _rot_table(pos0):
    half = 64
    inv = 1.0 / (10000.0 ** (np.arange(half, dtype=np.float64) * 2.0 / 128.0))
    pos = pos0 + np.arange(T, dtype=np.float64)
    ang = pos[None, :] * inv[:, None]
    c = np.cos(ang)
    s = np.sin(ang)
    C = np.concatenate([c, c], 0)
    S = np.concatenate([-s, s], 0)
    return np.stack([C, S], 0).astype(np.float32)


def _dft_table(S, k0):
    import ml_dtypes
    n = np.arange(S, dtype=np.int64)[:, None]
    out = np.empty((4, S, 2, TT), dtype=ml_dtypes.bfloat16)
    sc = 1.0 / math.sqrt(S)
    for kr in range(4):
        k = (k0 + kr * TT + np.arange(TT, dtype=np.int64))[None, :]
        ang = ((n * k) % S).astype(np.float64) * (2.0 * math.pi / S)
        out[kr, :, 0, :] = (np.cos(ang) * sc).astype(ml_dtypes.bfloat16)
        out[kr, :, 1, :] = (np.sin(ang) * sc).astype(ml_dtypes.bfloat16)
    return out


def _host_consts():
    import ml_dtypes
    if "shared" in _CONST_CACHE:
        return _CONST_CACHE["shared"]
    bf = ml_dtypes.bfloat16
    c = np.arange(128, dtype=np.float64)
    ang = 2.0 * math.pi * np.outer(c, c) / 128.0
    fc = np.concatenate([np.cos(ang), -np.sin(ang)], 1) / math.sqrt(128.0)
    m = np.arange(128, dtype=np.float64)[:, None]
    cc = np.arange(TT, dtype=np.float64)[None, :]
    j = np.arange(16, dtype=np.float64)[None, :]
    p = np.arange(128, dtype=np.float64)[:, None]
    out = {
        "ones_bf": np.ones((128, 128), dtype=bf),
        "ident_bf": np.eye(128, dtype=np.float32).astype(bf),
        "fc_bf": fc.astype(np.float32).astype(bf),
        "d0": (cc - m).astype(np.float32) * np.ones((128, 1), np.float32),
        "xidx": np.broadcast_to(cc + 1.0, (128, TT)).astype(np.float32).copy(),
        "tidx": np.broadcast_to(128.0 * j, (128, 16)).astype(np.float32).copy(),
        "zidx": np.stack([2047.0 - (128.0 * j + p), 128.0 * j + p], 1).astype(np.float32),
        "rot_p": _rot_table(0.0),
        "E_p": _dft_table(T, 0),
    }
    _CONST_CACHE["shared"] = out
    return out


def _core_consts(c):
    key = ("core", c)
    if key in _CONST_CACHE:
        return _CONST_CACHE[key]
    r = np.arange(NCORES, dtype=np.float64)
    jf = np.where(r < c, 2048.0 * (c - 1 - r), 0.0)
    mf = (r < c).astype(np.float64) * (128.0 ** -0.5)
    jb = np.where(r > c, 2048.0 * (r - c - 1), 0.0)
    mb = (r > c).astype(np.float64) * (128.0 ** -0.5)
    cidx = np.broadcast_to(np.stack([jf, mf, jb, mb], 0)[None], (128, 4, NCORES)).astype(np.float32).copy()
    out = {"cidx": cidx, "rot_s": _rot_table(float(c * T)), "E_s": _dft_table(SSAMP, c * T)}
    _CONST_CACHE[key] = out
    return out


def kernel(**inp):
    f32 = np.float32
    xp = np.asarray(inp["x_prompt"], f32)
    xs = np.asarray(inp["x_sample"], f32)
    mp = np.asarray(inp["mem_prompt"], f32)
    ms = np.asarray(inp["mem_sample"], f32)
    gl = []
    for l in range(DEPTH):
        for n in ["ffn1_norm", "mix_norm", "xa_norm", "mem_norm", "ffn2_norm"]:
            gl.append(np.asarray(inp[n], f32)[l])
    gl.append(np.asarray(inp["final_norm"], f32))
    gains = np.ascontiguousarray(np.stack(gl, 0).reshape(21, KC, 128).transpose(2, 0, 1).reshape(128, 21 * KC))
    decay = np.concatenate([np.asarray(inp["ret_decay_fwd"], f32).reshape(-1),
                            np.asarray(inp["ret_decay_bwd"], f32).reshape(-1)])[None, :]
    shared = {w: np.ascontiguousarray(np.asarray(inp[w], f32)) for w in WNAMES}
    shared["gains"] = gains
    shared["decay"] = np.ascontiguousarray(decay)
    shared.update(_host_consts())
    in_maps = []
    for c in range(NCORES):
        m = dict(shared)
        m.update(_core_consts(c))
        m["xT_p"] = np.ascontiguousarray(xp[c].T)
        m["xT_s"] = np.ascontiguousarray(xs[0, c * T:(c + 1) * T].T)
        m["memT_p"] = np.ascontiguousarray(mp[c].T)
        m["memT_s"] = np.ascontiguousarray(ms[0].T)
        in_maps.append(m)
    nc = build_nc()
    res = run_bass_kernel_spmd(nc, in_maps, core_ids=list(range(NCORES)))
    yp = np.stack([res.results[c]["yT_p"].T for c in range(NCORES)], 0).astype(f32)
    ys = np.concatenate([res.results[c]["yT_s"].T for c in range(NCORES)], 0)[None].astype(f32)
    return (np.ascontiguousarray(yp), np.ascontiguousarray(ys))
```
